# Optimizing a Trainium2 kernel written in Bass

```python
import math
import jax
import jax.numpy as jnp
from jax import lax
import numpy as np

D_MODEL = 4096
BATCH = 2
SEQ = 4096
DEPTH = 4

N_MIXERS = 2
N_S5_LAYERS = (DEPTH + N_MIXERS - 1) // N_MIXERS
N_HGRN_LAYERS = DEPTH // N_MIXERS
S5_GROUP = 16
S5_GROUPS = D_MODEL // S5_GROUP
S5_STATE = 64
S5_DT_MIN = 1e-3
S5_DT_MAX = 1e-1
S5_EIG_CLIP = 1e-4
HGRN_HEAD_DIM = 128
HGRN_HEADS = D_MODEL // HGRN_HEAD_DIM
HGRN_CHUNK = 64
D_FF = -(-8 * D_MODEL // (3 * 256)) * 256
RMS_EPS = 1e-6

kernel_name = "hybrid_s5_hgrn2_sandwich_trunk"


def rms_norm(x, gain):
    xf = x.astype(jnp.float32)
    y = xf * lax.rsqrt(jnp.mean(xf * xf, axis=-1, keepdims=True) + RMS_EPS)
    return (y * gain.astype(jnp.float32)).astype(x.dtype)


def _complex_affine_combine(left, right):
    ar1, ai1, br1, bi1 = left
    ar2, ai2, br2, bi2 = right
    return (ar2 * ar1 - ai2 * ai1,
            ar2 * ai1 + ai2 * ar1,
            ar2 * br1 - ai2 * bi1 + br2,
            ar2 * bi1 + ai2 * br1 + bi2)


def s5_mixer(u, a_re, a_im, log_dt, b_re, b_im, c_re, c_im, d_skip, w_glu):
    f32 = jnp.float32
    bsz, t, d = u.shape
    lam_re = jnp.minimum(a_re.astype(f32), -S5_EIG_CLIP)
    lam_im = a_im.astype(f32)
    dt = jnp.exp(log_dt.astype(f32))[:, None]
    mag = jnp.exp(lam_re * dt)
    abar_re = mag * jnp.cos(lam_im * dt)
    abar_im = mag * jnp.sin(lam_im * dt)
    denom = lam_re * lam_re + lam_im * lam_im
    z_re = ((abar_re - 1.0) * lam_re + abar_im * lam_im) / denom
    z_im = (abar_im * lam_re - (abar_re - 1.0) * lam_im) / denom
    br = b_re.astype(f32)
    bi = b_im.astype(f32)
    bbar_re = z_re[..., None] * br - z_im[..., None] * bi
    bbar_im = z_re[..., None] * bi + z_im[..., None] * br
    ug = u.astype(f32).reshape(bsz, t, S5_GROUPS, S5_GROUP)
    bu_re = jnp.einsum('btgh,gph->btgp', ug, bbar_re)
    bu_im = jnp.einsum('btgh,gph->btgp', ug, bbar_im)
    a_seq_re = jnp.broadcast_to(abar_re, (1, t, S5_GROUPS, S5_STATE))
    a_seq_im = jnp.broadcast_to(abar_im, (1, t, S5_GROUPS, S5_STATE))
    _, _, x_re, x_im = lax.associative_scan(
        _complex_affine_combine, (a_seq_re, a_seq_im, bu_re, bu_im), axis=1)
    y = (jnp.einsum('btgp,ghp->btgh', x_re, c_re.astype(f32))
         - jnp.einsum('btgp,ghp->btgh', x_im, c_im.astype(f32))).reshape(bsz, t, d)
    y = y + d_skip.astype(f32) * u.astype(f32)
    y = jax.nn.gelu(y).astype(u.dtype)
    val, gate = jnp.split(y @ w_glu, 2, axis=-1)
    return (val * jax.nn.sigmoid(gate)).astype(u.dtype)


def hgrn2_mixer(h, w_in, lower_bound, g_norm, w_out):
    f32 = jnp.float32
    bsz, t, d = h.shape
    n_chunks = t // HGRN_CHUNK
    q, f, v, g = jnp.split(h @ w_in, 4, axis=-1)
    q = jax.nn.silu(q.astype(f32))
    f = f.astype(f32)
    lb = lower_bound.astype(f32)
    log_forget = jnp.logaddexp(jnp.log(lb), jnp.log1p(-lb) + jax.nn.log_sigmoid(f))
    k = (1.0 - lb) * jax.nn.sigmoid(-f)
    v = v.astype(f32)

    def to_chunks(z):
        return z.reshape(bsz, n_chunks, HGRN_CHUNK, HGRN_HEADS, HGRN_HEAD_DIM).transpose(1, 0, 3, 2, 4)

    causal = jnp.tril(jnp.ones((HGRN_CHUNK, HGRN_CHUNK), dtype=bool))

    def chunk_step(state, inp):
        qc, kc, vc, lc = inp
        cum = jnp.cumsum(lc, axis=2)
        diff = cum[:, :, :, None, :] - cum[:, :, None, :, :]
        decay = jnp.exp(jnp.where(causal[:, :, None], diff, -jnp.inf))
        scores = jnp.einsum('bhtd,bhsd,bhtsd->bhts', qc, kc, decay)
        out = (jnp.einsum('bhts,bhse->bhte', scores, vc)
               + jnp.einsum('bhtd,bhde->bhte', qc * jnp.exp(cum), state))
        cum_last = cum[:, :, -1:, :]
        state = (jnp.exp(cum_last[:, :, 0, :])[..., None] * state
                 + jnp.einsum('bhsd,bhse->bhde', kc * jnp.exp(cum_last - cum), vc))
        return state, out

    state0 = jnp.zeros((bsz, HGRN_HEADS, HGRN_HEAD_DIM, HGRN_HEAD_DIM), f32)
    _, o = lax.scan(chunk_step, state0,
                    (to_chunks(q), to_chunks(k), to_chunks(v), to_chunks(log_forget)))
    o = o.transpose(1, 0, 3, 2, 4).reshape(bsz, t, HGRN_HEADS, HGRN_HEAD_DIM)
    o = o * lax.rsqrt(jnp.mean(o * o, axis=-1, keepdims=True) + RMS_EPS)
    o = o.reshape(bsz, t, d) * g_norm.astype(f32) * jax.nn.silu(g.astype(f32))
    return o.astype(h.dtype) @ w_out


def swiglu_ffn(h, w_gate_up, w_down):
    gate, up = jnp.split(h @ w_gate_up, 2, axis=-1)
    return (jax.nn.silu(gate) * up) @ w_down


def setup_inputs(seed: int = 0) -> dict:
    key = jax.random.key(seed)
    ks = jax.random.split(key, 18)
    f32 = jnp.float32
    nrm = lambda k, shape, scale: scale * jax.random.normal(k, shape, f32)
    x = nrm(ks[0], (BATCH, SEQ, D_MODEL), 1.0)
    norm_gains = 1.0 + nrm(ks[1], (DEPTH, 4, D_MODEL), 0.05)
    s5_a_re = -0.5 + nrm(ks[2], (N_S5_LAYERS, S5_GROUPS, S5_STATE), 0.01)
    s5_a_im = (jnp.pi * jnp.arange(S5_STATE, dtype=f32)
               + nrm(ks[3], (N_S5_LAYERS, S5_GROUPS, S5_STATE), 0.01))
    s5_log_dt = jax.random.uniform(ks[4], (N_S5_LAYERS, S5_GROUPS), f32,
                                   minval=math.log(S5_DT_MIN), maxval=math.log(S5_DT_MAX))
    b_scale = (2.0 * S5_GROUP) ** -0.5
    s5_b_re = nrm(ks[5], (N_S5_LAYERS, S5_GROUPS, S5_STATE, S5_GROUP), b_scale)
    s5_b_im = nrm(ks[6], (N_S5_LAYERS, S5_GROUPS, S5_STATE, S5_GROUP), b_scale)
    c_scale = (2.0 * S5_STATE) ** -0.5
    s5_c_re = nrm(ks[7], (N_S5_LAYERS, S5_GROUPS, S5_GROUP, S5_STATE), c_scale)
    s5_c_im = nrm(ks[8], (N_S5_LAYERS, S5_GROUPS, S5_GROUP, S5_STATE), c_scale)
    s5_d = nrm(ks[9], (N_S5_LAYERS, D_MODEL), 1.0)
    s5_w_glu = nrm(ks[10], (N_S5_LAYERS, D_MODEL, 2 * D_MODEL), D_MODEL ** -0.5)
    hgrn_w_in = nrm(ks[11], (N_HGRN_LAYERS, D_MODEL, 4 * D_MODEL), D_MODEL ** -0.5)
    hgrn_lb_logits = nrm(ks[12], (DEPTH, D_MODEL), 0.1)
    hgrn_g_norm = 1.0 + nrm(ks[13], (N_HGRN_LAYERS, D_MODEL), 0.05)
    hgrn_w_out = nrm(ks[14], (N_HGRN_LAYERS, D_MODEL, D_MODEL), D_MODEL ** -0.5)
    ffn_w_gate_up = nrm(ks[15], (DEPTH, D_MODEL, 2 * D_FF), D_MODEL ** -0.5)
    ffn_w_down = nrm(ks[16], (DEPTH, D_FF, D_MODEL), D_FF ** -0.5)
    return {"x": x, "norm_gains": norm_gains,
            "s5_a_re": s5_a_re, "s5_a_im": s5_a_im, "s5_log_dt": s5_log_dt,
            "s5_b_re": s5_b_re, "s5_b_im": s5_b_im, "s5_c_re": s5_c_re, "s5_c_im": s5_c_im,
            "s5_d": s5_d, "s5_w_glu": s5_w_glu,
            "hgrn_w_in": hgrn_w_in, "hgrn_lb_logits": hgrn_lb_logits,
            "hgrn_g_norm": hgrn_g_norm, "hgrn_w_out": hgrn_w_out,
            "ffn_w_gate_up": ffn_w_gate_up, "ffn_w_down": ffn_w_down}


def reference(x, norm_gains, s5_a_re, s5_a_im, s5_log_dt, s5_b_re, s5_b_im, s5_c_re, s5_c_im,
              s5_d, s5_w_glu, hgrn_w_in, hgrn_lb_logits, hgrn_g_norm, hgrn_w_out,
              ffn_w_gate_up, ffn_w_down):
    lb_probs = jax.nn.softmax(hgrn_lb_logits.astype(jnp.float32), axis=0)
    lb_cum = jnp.cumsum(lb_probs, axis=0)
    lower_bounds = lb_cum - lb_cum[0]
    h = x
    for layer in range(DEPTH):
        gains = norm_gains[layer]
        a = rms_norm(h, gains[0])
        j = layer // N_MIXERS
        if layer % N_MIXERS == 0:
            m = s5_mixer(a, s5_a_re[j], s5_a_im[j], s5_log_dt[j], s5_b_re[j], s5_b_im[j],
                         s5_c_re[j], s5_c_im[j], s5_d[j], s5_w_glu[j])
        else:
            m = hgrn2_mixer(a, hgrn_w_in[j], lower_bounds[layer], hgrn_g_norm[j], hgrn_w_out[j])
        h = h + rms_norm(m, gains[1])
        f = swiglu_ffn(rms_norm(h, gains[2]), ffn_w_gate_up[layer], ffn_w_down[layer])
        h = h + rms_norm(f, gains[3])
    return h
```

```python
import contextlib
import math
import numpy as np
import concourse.bass as bass
import concourse.mybir as mybir
from concourse.bass_utils import run_bass_kernel_spmd

F32 = mybir.dt.float32
BF16 = mybir.dt.bfloat16
AF = mybir.ActivationFunctionType
ALU = mybir.AluOpType
ENGS = ("pe", "dve", "act", "pool", "sp")
PI = math.pi
import os
NOWAIT = bool(int(os.environ.get('NOWAIT', '0')))
ONLY = os.environ.get('ONLY', '')
DEBUG = bool(int(os.environ.get('KDEBUG', '0')))
KSTOP = int(os.environ.get('KSTOP', '-1'))
EPS = 1e-6
HG_CH = 64


class Op:
    __slots__ = ("eng", "fn", "dma_key", "deps", "milestone", "sig_val", "is_dma")

    def __init__(self, eng, fn, dma_key):
        self.eng = eng
        self.fn = fn
        self.dma_key = dma_key
        self.is_dma = dma_key is not None
        self.deps = []
        self.milestone = False
        self.sig_val = None


class Sched:
    def __init__(self, nc):
        self.nc = nc
        self.ops = {e: [] for e in ENGS}
        self.last_writer = {}
        self.readers = {}
        self.dma_count = {}
        self.last_dma = {}
        self.all_ops = []
        self.pending = {e: [] for e in ENGS}

    def barrier(self):
        deps = []
        for e in ENGS:
            if self.ops[e]:
                deps.append(self.ops[e][-1])
        deps.extend(self.last_dma.values())
        for e in ENGS:
            self.pending[e] = list(deps)
        self.last_writer = {}
        self.readers = {}

    def add(self, eng, fn, reads=(), writes=(), dma_key=None):
        op = Op(eng, fn, dma_key)
        deps = list(self.pending[eng])
        self.pending[eng] = []
        for r in reads:
            lw = self.last_writer.get(r)
            if lw is not None:
                deps.append(lw)
        for w in writes:
            lw = self.last_writer.get(w)
            if lw is not None:
                deps.append(lw)
            deps.extend(self.readers.get(w, ()))
        if op.is_dma and dma_key in self.last_dma:
            deps.append(self.last_dma[dma_key])
        for r in reads:
            self.readers.setdefault(r, []).append(op)
        for w in writes:
            self.last_writer[w] = op
            self.readers[w] = []
        seen = set()
        for d in deps:
            if d is op or id(d) in seen:
                continue
            seen.add(id(d))
            if d.eng == "pe" and eng == "pe" and not d.is_dma and not op.is_dma:
                continue
            op.deps.append(d)
        if op.is_dma:
            c = self.dma_count.get(dma_key, 0) + 1
            self.dma_count[dma_key] = c
            op.sig_val = 16 * c
            self.last_dma[dma_key] = op
        self.ops[eng].append(op)
        self.all_ops.append(op)
        return op

    def setup(self, stack):
        self.stack = stack
        self.esem = {e: stack.enter_context(self.nc.semaphore("s_" + e)) for e in ENGS}
        self.dsem = {}
        self.counters = {e: 0 for e in ENGS}
        self.seen_e = {e: {x: 0 for x in ENGS} for e in ENGS}
        self.seen_d = {e: {} for e in ENGS}
        self.cur = {e: [] for e in ENGS}
        self.emitted = 0

    def flush(self, final_dma_keys=()):
        nc = self.nc
        new_ops = {e: self.ops[e][len(self.cur[e]):] for e in ENGS}
        for e in ENGS:
            for op in new_ops[e]:
                for d in op.deps:
                    if not d.is_dma:
                        d.milestone = True
            if new_ops[e]:
                last = new_ops[e][-1]
                if not last.is_dma:
                    last.milestone = True
        for e in ENGS:
            for op in new_ops[e]:
                if op.milestone and not op.is_dma:
                    self.counters[e] += 1
                    op.sig_val = self.counters[e]
                if op.is_dma and op.dma_key not in self.dsem:
                    self.dsem[op.dma_key] = self.stack.enter_context(nc.semaphore("d_%d" % len(self.dsem)))
        esem, dsem = self.esem, self.dsem
        with nc.Block() as block:
            def run(e, engobj):
                seen_e = self.seen_e[e]
                seen_d = self.seen_d[e]
                for op in new_ops[e]:
                    for d in op.deps:
                        if d.is_dma:
                            if seen_d.get(d.dma_key, 0) < d.sig_val:
                                engobj.wait_ge(dsem[d.dma_key], d.sig_val)
                                seen_d[d.dma_key] = d.sig_val
                        else:
                            if seen_e[d.eng] < d.sig_val:
                                engobj.wait_ge(esem[d.eng], d.sig_val)
                                seen_e[d.eng] = d.sig_val
                    ins = op.fn(engobj)
                    op.fn = None
                    if op.is_dma:
                        ins.then_inc(dsem[op.dma_key], 16)
                    elif op.milestone:
                        ins.then_inc(esem[e], 1)
                if e == "sp":
                    for k in final_dma_keys:
                        engobj.wait_ge(dsem[k], 16 * self.dma_count[k])

            @block.tensor
            def _(eng):
                run("pe", eng)

            @block.vector
            def _(eng):
                run("dve", eng)

            @block.scalar
            def _(eng):
                run("act", eng)

            @block.gpsimd
            def _(eng):
                run("pool", eng)

            @block.sync
            def _(eng):
                run("sp", eng)
        for e in ENGS:
            self.cur[e] = list(self.ops[e])
        self.barrier()


class Cfg:
    def __init__(self, D=4096, T=4096, DFF=11008, DEPTH=4):
        self.D, self.T, self.DFF, self.DEPTH = D, T, DFF, DEPTH
        self.KC = D // 128
        self.TT = 512
        self.NT = T // 512
        self.JF = DFF // 128
        self.G = D // 16
        self.NS5 = (DEPTH + 1) // 2
        self.NHG = DEPTH // 2
        self.ARENA = 48 * 1024


def build_program(cfg):
    D, T, KC, TT, NT, JF, G, DEPTH = cfg.D, cfg.T, cfg.KC, cfg.TT, cfg.NT, cfg.JF, cfg.G, cfg.DEPTH
    NS5, NHG = cfg.NS5, cfg.NHG
    CH = HG_CH
    NCK = T // CH
    nc = bass.Bass("TRN2", target_bir_lowering=False)

    def din(name, shape, dt=F32):
        return nc.dram_tensor(name, list(shape), dt, kind="ExternalInput").ap()

    def dscr(name, shape, dt=F32):
        if DEBUG:
            return nc.dram_tensor(name, list(shape), dt, kind="ExternalOutput").ap()
        return nc.dram_tensor(name, list(shape), dt).ap()

    xT = din("xT", [KC, 128, T])
    gains_d = din("gains", [128, DEPTH * 4 * KC])
    cst_d = din("cst", [128, 5 * 128 + 8 + 1 + 512])
    cmask_d = din("cmask", [128, T])
    lbl_d = din("lbl", [128, DEPTH * KC])
    s5_a1 = din("s5_a1", [NS5, 3, 128, G])
    s5_a2 = din("s5_a2", [NS5, KC, 3, 128, 64])
    s5_b2 = din("s5_b2", [NS5, KC, 2, 128, 64])
    s5_c2 = din("s5_c2", [NS5, KC, 2, 128, 128])
    s5_dd = din("s5_dd", [NS5, 128, KC])
    s5_wg = din("s5_wg", [NS5, KC, 128, KC, 2, 128])
    hg_win = din("hg_win", [NHG, 4 * KC, 128, KC, 128])
    hg_wout = din("hg_wout", [NHG, KC, 128, KC, 128])
    hg_gn = din("hg_gn", [NHG, 128, KC])
    ff_gu = din("ff_gu", [DEPTH, JF, 128, KC, 2, 128])
    ff_dn = din("ff_dn", [DEPTH, KC, 128, JF, 128])
    outT = nc.dram_tensor("outT", [KC, 128, T], F32, kind="ExternalOutput").ap()

    hT = dscr("hT", [KC, 128, T])
    uT = dscr("uT", [KC, 128, T], BF16)
    yT = dscr("yT", [KC, 128, T], BF16)
    qfv = dscr("qfv", [4 * KC, 128, T])
    onT = dscr("onT", [KC, 128, T])
    msc = dscr("msc", [KC, 128, TT])

    with contextlib.ExitStack() as st:
        cst = st.enter_context(nc.sbuf_tensor("cstsb", [128, 5 * 128 + 8 + 1 + 512], F32))
        gains = st.enter_context(nc.sbuf_tensor("gainsb", [128, DEPTH * 4 * KC], F32))
        cbf = st.enter_context(nc.sbuf_tensor("cbf", [128, 2 * 128], BF16))
        lbt = st.enter_context(nc.sbuf_tensor("lbt", [128, 3 * DEPTH * KC], F32))
        PSF = [st.enter_context(nc.psum_tensor("psf%d" % i, [128, 512], F32)) for i in range(6)]
        PSB = [st.enter_context(nc.psum_tensor("psb%d" % i, [128, 1024], BF16)) for i in range(2)]

        S = Sched(nc)
        S.setup(st)
        uid = [0]
        ident = cst[:, 0:128]
        swapm = cst[:, 128:256]
        maskT = cst[:, 256:384]
        ones32 = cst[:, 384:512]
        mask8 = cst[:, 640:648]
        sgn = cst[:, 648:649]
        iota1 = cst[:, 649:649 + 512]
        identb = cbf[:, 0:128]
        onesb = cbf[:, 128:256]

        PH = [None]

        def phase_begin():
            PH[0] = contextlib.ExitStack()

        def phase_end(final=()):
            S.flush(final_dma_keys=final)
            PH[0].close()
            PH[0] = None

        def A32(n):
            uid[0] += 1
            return PH[0].enter_context(nc.sbuf_tensor("t%d" % uid[0], [128, n], F32))[:]

        def A16(n):
            uid[0] += 1
            return PH[0].enter_context(nc.sbuf_tensor("t%d" % uid[0], [128, n], BF16))[:]

        def AI32(n):
            uid[0] += 1
            return PH[0].enter_context(nc.sbuf_tensor("t%d" % uid[0], [128, n], mybir.dt.int32))[:]

        def key(prefix):
            uid[0] += 1
            return "%s%d" % (prefix, uid[0])

        S.add("sp", lambda e: e.dma_start(out=cst[:], in_=cst_d), writes=["cst"], dma_key="cst")
        S.add("sp", lambda e: e.dma_start(out=gains[:], in_=gains_d), writes=["gains"], dma_key="gains")
        S.add("sp", lambda e: e.dma_start(out=lbt[:, 0:DEPTH * KC], in_=lbl_d), writes=["lbl"], dma_key="lbl")
        S.add("dve", lambda e: e.tensor_copy(out=identb, in_=ident), reads=["cst"], writes=["identb"])
        S.add("dve", lambda e: e.tensor_copy(out=onesb, in_=ones32), reads=["cst"], writes=["onesb"])
        lbe = lbt[:, 0:DEPTH * KC]
        lbv = lbt[:, DEPTH * KC:2 * DEPTH * KC]
        omv = lbt[:, 2 * DEPTH * KC:3 * DEPTH * KC]
        S.add("act", lambda e: e.activation(out=lbe, in_=lbe, func=AF.Exp), reads=["lbl"], writes=["lbl"])
        def _lbsum(e):
            ins = e.tensor_copy(out=omv[:, 0:KC], in_=lbe[:, 0:KC])
            return ins
        S.add("dve", _lbsum, reads=["lbl"], writes=["om0"])
        for l in range(1, DEPTH):
            S.add("dve", lambda e, l=l: e.tensor_tensor(out=omv[:, 0:KC], in0=omv[:, 0:KC], in1=lbe[:, l * KC:(l + 1) * KC], op=ALU.add),
                  reads=["lbl", "om0"], writes=["om0"])
        S.add("dve", lambda e: e.reciprocal(out=omv[:, KC:2 * KC], in_=omv[:, 0:KC]), reads=["om0"], writes=["om1"])
        for l in range(DEPTH):
            S.add("dve", lambda e, l=l: e.tensor_tensor(out=lbe[:, l * KC:(l + 1) * KC], in0=lbe[:, l * KC:(l + 1) * KC],
                                                       in1=omv[:, KC:2 * KC], op=ALU.mult),
                  reads=["lbl", "om1"], writes=["lbl"])
        S.add("dve", lambda e: e.memset(lbv[:, 0:KC], 0.0), writes=["lbv"])
        for l in range(1, DEPTH):
            S.add("dve", lambda e, l=l: e.tensor_tensor(out=lbv[:, l * KC:(l + 1) * KC], in0=lbv[:, (l - 1) * KC:l * KC],
                                                       in1=lbe[:, l * KC:(l + 1) * KC], op=ALU.add),
                  reads=["lbl", "lbv"], writes=["lbv"])
        S.add("dve", lambda e: e.tensor_scalar(out=omv, in0=lbv, scalar1=-1.0, scalar2=1.0, op0=ALU.mult, op1=ALU.add),
              reads=["lbv", "om1"], writes=["omv"])

        def gcol(l, i, kc):
            o = (l * 4 + i) * KC + kc
            return gains[:, o:o + 1]

        def tslice(tt):
            return slice(tt * TT, (tt + 1) * TT)

        class Dense:
            def __init__(self, need_g):
                self.aT = [A16(TT) for _ in range(KC)]
                self.gT = [A16(TT) for _ in range(JF)] if need_g else None
                wsz = max(KC * 2 * 128, JF * 128)
                self.NW = 2
                self.w = [A16(wsz) for _ in range(self.NW)]
                self.wk = ["w%d" % i for i in range(self.NW)]
                self.wi = 0
                self.NH = 4
                self.hs = [A32(TT) for _ in range(self.NH)]
                self.hk = ["hs%d" % i for i in range(self.NH)]
                self.hi = 0
                self.ms = [A32(TT) for _ in range(self.NH)]
                self.mk = ["ms%d" % i for i in range(self.NH)]
                self.mi = 0
                self.sq = [A16(TT) for _ in range(2)]
                self.sqk = ["sq%d" % i for i in range(2)]
                self.sqi = 0
                self.sd = A32(TT)
                self.rstd = A32(TT)
                self.tmp = [A32(TT) for _ in range(2)]
                self.tk = ["tmp%d" % i for i in range(2)]
                self.ti = 0
                self.psi = 0
                self.aTk = "aT"
                self.gTk = "gT"
                self.ssk = "ss"
                self.rk = "rstd"

            def hslot(self):
                i = self.hi % self.NH
                self.hi += 1
                return self.hs[i], self.hk[i]

            def mslot(self):
                i = self.mi % self.NH
                self.mi += 1
                return self.ms[i], self.mk[i]

            def sqslot(self):
                i = self.sqi % 2
                self.sqi += 1
                return self.sq[i], self.sqk[i]

            def tslot(self):
                i = self.ti % 2
                self.ti += 1
                return self.tmp[i], self.tk[i]

            def wslot(self):
                i = self.wi % self.NW
                self.wi += 1
                return self.w[i], self.wk[i]

            def psum(self):
                i = self.psi % 4
                self.psi += 1
                return PSF[i], "psf%d" % i

        PS_SS, PS_SSK = PSF[4], "psf4"

        def load_chunk(Dn, src_ap, src_key):
            slot, k = Dn.hslot()
            S.add("sp", lambda e: e.dma_start(out=slot, in_=src_ap), reads=[src_key], writes=[k], dma_key=k)
            return slot, k

        def ss_accum(Dn, src, srck, first, last):
            sq, sqk = Dn.sqslot()
            S.add("act", lambda e: e.activation(out=sq, in_=src, func=AF.Square), reads=[srck], writes=[sqk])
            S.add("pe", lambda e: e.matmul(PS_SS[:], lhsT=onesb, rhs=sq, start=first, stop=last),
                  reads=[sqk, "onesb"], writes=[PS_SSK])

        def make_rstd(Dn, n_feat):
            S.add("act", lambda e: e.activation(out=Dn.sd, in_=PS_SS[:], func=AF.Sqrt, scale=1.0 / n_feat, bias=EPS),
                  reads=[PS_SSK], writes=[Dn.rk + "sd"])
            S.add("dve", lambda e: e.reciprocal(out=Dn.rstd, in_=Dn.sd), reads=[Dn.rk + "sd"], writes=[Dn.rk])

        def rmsnorm_to_aT(Dn, src, srcname, tt, l, gi):
            for kc in range(KC):
                slot, k = load_chunk(Dn, src[kc][:, tslice(tt)], (srcname, kc))
                ss_accum(Dn, slot, k, kc == 0, kc == KC - 1)
            make_rstd(Dn, D)
            for kc in range(KC):
                slot, k = load_chunk(Dn, src[kc][:, tslice(tt)], (srcname, kc))
                S.add("dve", lambda e, slot=slot, kc=kc: e.scalar_tensor_tensor(
                    out=Dn.aT[kc], in0=slot, scalar=gcol(l, gi, kc), in1=Dn.rstd, op0=ALU.mult, op1=ALU.mult),
                    reads=[k, Dn.rk, "gains"], writes=[(Dn.aTk, kc)])

        def linear(Dn, Wd, nch, kcin, pair, consumer, rhs_of=None, rkeys=None):
            if rhs_of is None:
                rhs_of = lambda kc: Dn.aT[kc]
                rkeys = lambda kc: (Dn.aTk, kc)
            for n in range(nch):
                w, wk = Dn.wslot()
                npair = 2 if pair else 1
                wv = w[:, 0:kcin * npair * 128]
                src = Wd[n]
                if pair:
                    src = src.rearrange("p k a c -> p (k a c)")
                    wv3 = wv.rearrange("p (k a c) -> p k a c", a=2, c=128)
                else:
                    src = src.rearrange("p k c -> p (k c)")
                    wv3 = wv.rearrange("p (k c) -> p k c", c=128)
                S.add("pool", lambda e, wv=wv, src=src: e.dma_start(out=wv, in_=src, max_dma_last_dim=8192),
                      writes=[wk], dma_key=wk)
                outs = []
                for a in range(npair):
                    ps, pk = Dn.psum()
                    for kc in range(kcin):
                        lhsT = wv3[:, kc, a, :] if pair else wv3[:, kc, :]
                        S.add("pe", lambda e, ps=ps, lhsT=lhsT, kc=kc: e.matmul(ps[:], lhsT=lhsT, rhs=rhs_of(kc),
                                                                               start=(kc == 0), stop=(kc == kcin - 1)),
                              reads=[wk, rkeys(kc)], writes=[pk])
                    outs.append((ps, pk))
                consumer(n, outs)

        def out_chunk(Dn, n, nlast, src, srck, already_sbuf=False):
            if already_sbuf:
                ms, mk = src, srck
            else:
                ms, mk = Dn.mslot()
                S.add("act", lambda e: e.activation(out=ms, in_=src, func=AF.Copy), reads=[srck], writes=[mk])
            ss_accum(Dn, ms, mk, n == 0, n == nlast)
            S.add("sp", lambda e: e.dma_start(out=msc[n], in_=ms), reads=[mk], writes=[("msc", n)], dma_key=mk + "st")

        def resid_update(Dn, tt, l, gi, dst, dstname):
            make_rstd(Dn, D)
            for kc in range(KC):
                ms, mk = Dn.mslot()
                S.add("sp", lambda e, ms=ms, kc=kc: e.dma_start(out=ms, in_=msc[kc]), reads=[("msc", kc)], writes=[mk],
                      dma_key=mk)
                hs, hk = load_chunk(Dn, hT[kc][:, tslice(tt)], ("hT", kc))
                S.add("dve", lambda e, ms=ms, kc=kc: e.scalar_tensor_tensor(
                    out=ms, in0=ms, scalar=gcol(l, gi, kc), in1=Dn.rstd, op0=ALU.mult, op1=ALU.mult),
                    reads=[mk, Dn.rk, "gains"], writes=[mk])
                S.add("pool", lambda e, ms=ms, hs=hs: e.tensor_tensor(out=hs, in0=hs, in1=ms, op=ALU.add),
                      reads=[mk, hk], writes=[hk])
                S.add("sp", lambda e, hs=hs, kc=kc: e.dma_start(out=dst[kc][:, tslice(tt)], in_=hs), reads=[hk],
                      writes=[(dstname, kc)], dma_key=hk + "st")

        def ffn_tile(Dn, tt, l, last_layer):
            rmsnorm_to_aT(Dn, hT, "hT", tt, l, 2)

            def gu_cons(j, outs):
                (pg, pgk), (pu, puk) = outs
                t, tk = Dn.tslot()
                S.add("act", lambda e: e.activation(out=t, in_=pg[:], func=AF.Silu), reads=[pgk], writes=[tk])
                S.add("dve", lambda e: e.tensor_tensor(out=Dn.gT[j], in0=pu[:], in1=t, op=ALU.mult),
                      reads=[puk, tk], writes=[(Dn.gTk, j)])
            linear(Dn, ff_gu[l], JF, KC, True, gu_cons)

            def dn_cons(m, outs):
                (pf, pfk), = outs
                out_chunk(Dn, m, KC - 1, pf[:], pfk)
            linear(Dn, ff_dn[l], KC, JF, False, dn_cons, rhs_of=lambda j: Dn.gT[j], rkeys=lambda j: (Dn.gTk, j))
            if last_layer:
                resid_update(Dn, tt, l, 3, outT, "outT")
            else:
                resid_update(Dn, tt, l, 3, hT, "hT")

        S.flush()
        phase_begin()
        cp = [A32(T) for _ in range(2)]
        cpk = ["cp0", "cp1"]
        for kc in range(KC):
            i = kc % 2
            S.add("sp", lambda e, i=i, kc=kc: e.dma_start(out=cp[i], in_=xT[kc]), writes=[cpk[i]], dma_key=cpk[i])
            S.add("sp", lambda e, i=i, kc=kc: e.dma_start(out=hT[kc], in_=cp[i]), reads=[cpk[i]], writes=[("hT", kc)],
                  dma_key=cpk[i] + "st")
        phase_end()

        def s5_phase(j):
            TB = 512
            NB = T // TB
            phase_begin()
            a1 = A32(3 * G).rearrange("p (a g) -> p a g", g=G)
            th1 = A32(G)
            r1 = A32(G)
            cosL = A32(G)
            ssinL = A32(G)
            t1g = A32(G)
            ddc = A32(KC)
            S.add("sp", lambda e: e.dma_start(out=a1, in_=s5_a1[j].rearrange("a p g -> p a g")), writes=["a1"], dma_key="a1")
            S.add("sp", lambda e: e.dma_start(out=ddc, in_=s5_dd[j]), writes=["ddc"], dma_key="ddc")
            S.add("act", lambda e: e.activation(out=a1[:, 2, :], in_=a1[:, 2, :], func=AF.Exp), reads=["a1"], writes=["a1"])
            S.add("dve", lambda e: e.tensor_tensor(out=th1, in0=a1[:, 1, :], in1=a1[:, 2, :], op=ALU.mult), reads=["a1"], writes=["th1"])
            S.add("dve", lambda e: e.scalar_tensor_tensor(out=r1, in0=a1[:, 0, :], scalar=-1e-4, in1=a1[:, 2, :], op0=ALU.min, op1=ALU.mult),
                  reads=["a1"], writes=["r1"])
            S.add("act", lambda e: e.activation(out=r1, in_=r1, func=AF.Exp), reads=["r1"], writes=["r1"])
            rr_f = A32(512)
            rr_i = AI32(512)

            def sin_of(out, x, n, rkeys, wkeys):
                f_ = rr_f[:, 0:n]
                i_ = rr_i[:, 0:n]
                RR = "rr"
                S.add("dve", lambda e: e.tensor_scalar(out=f_, in0=x, scalar1=1.0 / (2 * PI), scalar2=None, op0=ALU.mult),
                      reads=list(rkeys) + [RR], writes=[RR])
                S.add("dve", lambda e: e.tensor_copy(out=i_, in_=f_), reads=[RR], writes=[RR + "i"])
                S.add("dve", lambda e: e.tensor_copy(out=f_, in_=i_), reads=[RR + "i"], writes=[RR])
                S.add("dve", lambda e: e.scalar_tensor_tensor(out=x, in0=f_, scalar=-2 * PI, in1=x, op0=ALU.mult, op1=ALU.add),
                      reads=[RR] + list(rkeys), writes=list(rkeys))
                S.add("dve", lambda e: e.tensor_scalar(out=f_, in0=x, scalar1=PI, scalar2=2 * PI, op0=ALU.is_gt, op1=ALU.mult),
                      reads=list(rkeys) + [RR], writes=[RR])
                S.add("dve", lambda e: e.tensor_tensor(out=x, in0=x, in1=f_, op=ALU.subtract), reads=[RR] + list(rkeys), writes=list(rkeys))
                S.add("dve", lambda e: e.tensor_scalar(out=f_, in0=x, scalar1=-PI, scalar2=2 * PI, op0=ALU.is_lt, op1=ALU.mult),
                      reads=list(rkeys) + [RR], writes=[RR])
                S.add("dve", lambda e: e.tensor_tensor(out=x, in0=x, in1=f_, op=ALU.add), reads=[RR] + list(rkeys), writes=list(rkeys))
                S.add("act", lambda e: e.activation(out=out, in_=x, func=AF.Sin), reads=list(rkeys) + list(wkeys), writes=list(wkeys))

            S.add("dve", lambda e: e.tensor_scalar(out=t1g, in0=th1, scalar1=float(TB), scalar2=None, op0=ALU.mult),
                  reads=["th1"], writes=["t1g"])
            sin_of(ssinL, t1g, G, ["t1g"], ["ssinL"])
            S.add("dve", lambda e: e.tensor_scalar(out=ssinL, in0=ssinL, scalar1=sgn, scalar2=None, op0=ALU.mult),
                  reads=["ssinL", "cst"], writes=["ssinL"])
            S.add("dve", lambda e: e.tensor_scalar(out=t1g, in0=th1, scalar1=float(TB), scalar2=0.5 * PI, op0=ALU.mult, op1=ALU.add),
                  reads=["th1", "ssinL", "t1g"], writes=["t1g"])
            sin_of(cosL, t1g, G, ["t1g"], ["cosL"])

            u = [A16(T) for _ in range(2)]
            uk = ["u0", "u1"]
            yb = [A16(T) for _ in range(2)]
            ybk = ["yb0", "yb1"]
            a2 = A32(3 * 64).rearrange("p (a q) -> p a q", q=64)
            b2 = A32(2 * 64).rearrange("p (a q) -> p a q", q=64)
            c2 = A32(2 * 128).rearrange("p (a q) -> p a q", q=128)
            wk_ = [A32(64) for _ in range(12)]
            Bf = A32(128)
            Bf2 = A32(128)
            CA = A32(128)
            CB = A32(128)
            ZB = A16(8 * 128).rearrange("p (g c) -> p g c", c=128)
            ZB2 = A16(8 * 128).rearrange("p (g c) -> p g c", c=128)
            ZCA = A16(8 * 128).rearrange("p (g c) -> p g c", c=128)
            ZCB = A16(8 * 128).rearrange("p (g c) -> p g c", c=128)
            ROT = A32(8 * 128).rearrange("p (g c) -> p g c", c=128)
            COS = A32(8 * TB).rearrange("p (g t) -> p g t", t=TB)
            SIN = A32(8 * TB).rearrange("p (g t) -> p g t", t=TB)
            RT = A32(8 * TB).rearrange("p (g t) -> p g t", t=TB)
            targ = A32(TB)
            st8 = A32(8)
            t1 = [A32(TB) for _ in range(2)]
            t2 = [A32(TB) for _ in range(2)]
            vt = [A32(TB) for _ in range(2)]
            vv = [A32(TB) for _ in range(2)]
            w1 = [A16(TB) for _ in range(2)]
            w2 = [A16(TB) for _ in range(2)]
            yy = A32(TB)
            gsq = A32(TB)
            gin = A32(TB)
            bk = [["s5b%d_%d" % (a_, b_) for b_ in range(8)] for a_ in range(2)]
            PS1 = [(PSF[0], "psf0"), (PSF[1], "psf1")]
            PS2 = [(PSF[2], "psf2"), (PSF[3], "psf3")]
            PSY = [(PSF[4], "psf4"), (PSF[5], "psf5")]
            PST = (PSB[0], "psb0")
            pst32 = PSB[0][:].bitcast(F32)

            it = 0
            for c8 in range(KC):
                ui = c8 % 2
                S.add("sp", lambda e, ui=ui, c8=c8: e.dma_start(out=u[ui], in_=uT[c8]), reads=[("uT", c8)], writes=[uk[ui]],
                      dma_key=uk[ui])
                S.add("sp", lambda e, c8=c8: e.dma_start(out=a2, in_=s5_a2[j, c8].rearrange("a p q -> p a q")), writes=["a2"], dma_key="a2")
                S.add("sp", lambda e, c8=c8: e.dma_start(out=b2, in_=s5_b2[j, c8].rearrange("a p q -> p a q")), writes=["b2"], dma_key="b2")
                S.add("sp", lambda e, c8=c8: e.dma_start(out=c2, in_=s5_c2[j, c8].rearrange("a p q -> p a q")), writes=["c2"], dma_key="c2")
                lam_re, dt_, x1, mag, th, ang, sn, cs, den, ere, zre, zim = wk_
                P = "s5p"
                def V(fn, eng="dve", r=(P,), w=(P,)):
                    S.add(eng, fn, reads=list(r), writes=list(w))
                V(lambda e: e.tensor_scalar(out=lam_re, in0=a2[:, 0, :], scalar1=-1e-4, scalar2=None, op0=ALU.min), r=("a2", P))
                V(lambda e: e.activation(out=dt_, in_=a2[:, 2, :], func=AF.Exp), eng="act", r=("a2", P))
                V(lambda e: e.tensor_tensor(out=x1, in0=lam_re, in1=dt_, op=ALU.mult))
                V(lambda e: e.activation(out=mag, in_=x1, func=AF.Exp), eng="act")
                V(lambda e: e.tensor_tensor(out=th, in0=a2[:, 1, :], in1=dt_, op=ALU.mult), r=("a2", P))
                V(lambda e: e.tensor_copy(out=ang, in_=th))
                sin_of(sn, ang, 64, [P], [P])
                V(lambda e: e.tensor_scalar(out=ang, in0=th, scalar1=0.5 * PI, scalar2=None, op0=ALU.add))
                sin_of(cs, ang, 64, [P], [P])
                V(lambda e: e.tensor_tensor(out=sn, in0=sn, in1=mag, op=ALU.mult))
                V(lambda e: e.tensor_tensor(out=cs, in0=cs, in1=mag, op=ALU.mult))
                V(lambda e: e.tensor_tensor(out=den, in0=lam_re, in1=lam_re, op=ALU.mult))
                V(lambda e: e.tensor_tensor(out=x1, in0=a2[:, 1, :], in1=a2[:, 1, :], op=ALU.mult), r=("a2", P))
                V(lambda e: e.tensor_tensor(out=den, in0=den, in1=x1, op=ALU.add))
                V(lambda e: e.reciprocal(out=den, in_=den))
                V(lambda e: e.tensor_scalar(out=ere, in0=cs, scalar1=-1.0, scalar2=None, op0=ALU.add))
                V(lambda e: e.tensor_tensor(out=zre, in0=ere, in1=lam_re, op=ALU.mult))
                V(lambda e: e.tensor_tensor(out=x1, in0=sn, in1=a2[:, 1, :], op=ALU.mult), r=("a2", P))
                V(lambda e: e.tensor_tensor(out=zre, in0=zre, in1=x1, op=ALU.add))
                V(lambda e: e.tensor_tensor(out=zre, in0=zre, in1=den, op=ALU.mult))
                V(lambda e: e.tensor_tensor(out=zim, in0=sn, in1=lam_re, op=ALU.mult))
                V(lambda e: e.tensor_tensor(out=x1, in0=ere, in1=a2[:, 1, :], op=ALU.mult), r=("a2", P))
                V(lambda e: e.tensor_tensor(out=zim, in0=zim, in1=x1, op=ALU.subtract))
                V(lambda e: e.tensor_tensor(out=zim, in0=zim, in1=den, op=ALU.mult))
                V(lambda e: e.tensor_tensor(out=Bf[:, 0:64], in0=zre, in1=b2[:, 0, :], op=ALU.mult), r=("b2", P), w=("Bf",))
                V(lambda e: e.tensor_tensor(out=x1, in0=zim, in1=b2[:, 1, :], op=ALU.mult), r=("b2", P))
                V(lambda e: e.tensor_tensor(out=Bf[:, 0:64], in0=Bf[:, 0:64], in1=x1, op=ALU.subtract), r=("Bf", P), w=("Bf",))
                V(lambda e: e.tensor_tensor(out=Bf[:, 64:128], in0=zre, in1=b2[:, 1, :], op=ALU.mult), r=("b2", P, "Bf"), w=("Bf",))
                V(lambda e: e.tensor_tensor(out=x1, in0=zim, in1=b2[:, 0, :], op=ALU.mult), r=("b2", P, "Bf"))
                V(lambda e: e.tensor_tensor(out=Bf[:, 64:128], in0=Bf[:, 64:128], in1=x1, op=ALU.add), r=("Bf", P), w=("Bf",))
                V(lambda e: e.tensor_copy(out=Bf2[:, 0:64], in_=Bf[:, 64:128]), r=("Bf",), w=("Bf2",))
                V(lambda e: e.tensor_scalar(out=Bf2[:, 64:128], in0=Bf[:, 0:64], scalar1=-1.0, scalar2=None, op0=ALU.mult), r=("Bf", "Bf2"), w=("Bf2",))
                V(lambda e: e.tensor_scalar(out=CA, in0=c2[:, 0, :], scalar1=sgn, scalar2=None, op0=ALU.mult), r=("c2", "cst"), w=("CA",))
                V(lambda e: e.tensor_scalar(out=CB, in0=c2[:, 1, :], scalar1=-1.0, scalar2=None, op0=ALU.mult), r=("c2",), w=("CB",))
                for g_ in range(8):
                    g = c8 * 8 + g_
                    V(lambda e, g_=g_: e.tensor_scalar(out=ZB[:, g_, :], in0=Bf, scalar1=mask8[:, g_:g_ + 1], scalar2=None, op0=ALU.mult),
                      r=("Bf", "cst"), w=("Z",))
                    V(lambda e, g_=g_: e.tensor_scalar(out=ZB2[:, g_, :], in0=Bf2, scalar1=mask8[:, g_:g_ + 1], scalar2=None, op0=ALU.mult),
                      r=("Bf2", "cst", "Z"), w=("Z",))
                S.add("pool", lambda e: e.memset(ZCA, 0.0), reads=["Z"], writes=["ZC"])
                S.add("pool", lambda e: e.memset(ZCB, 0.0), reads=["ZC"], writes=["ZC"])
                for g_ in range(8):
                    g = c8 * 8 + g_
                    cs_ = slice(g_ * 16, g_ * 16 + 16)
                    V(lambda e, g_=g_, cs_=cs_: e.tensor_copy(out=ZCA[:, g_, cs_], in_=CA[:, cs_]), r=("CA", "ZC"), w=("ZC",))
                    V(lambda e, g_=g_, cs_=cs_: e.tensor_copy(out=ZCB[:, g_, cs_], in_=CB[:, cs_]), r=("CB", "ZC"), w=("ZC",))
                    V(lambda e, g_=g_, g=g: e.tensor_scalar(out=ROT[:, g_, :], in0=ident, scalar1=cosL[:, g:g + 1], scalar2=None, op0=ALU.mult),
                      r=("cst", "cosL", "ROT"), w=("ROT",))
                    V(lambda e, g_=g_, g=g: e.scalar_tensor_tensor(out=ROT[:, g_, :], in0=swapm, scalar=ssinL[:, g:g + 1], in1=ROT[:, g_, :],
                                                                  op0=ALU.mult, op1=ALU.add),
                      r=("cst", "ssinL", "ROT"), w=("ROT",))
                    V(lambda e, g=g: e.tensor_scalar(out=targ, in0=iota1, scalar1=th1[:, g:g + 1], scalar2=None, op0=ALU.mult),
                      r=("cst", "th1", "targ", "TAB"), w=("targ",))
                    sin_of(SIN[:, g_, :], targ, TB, ["targ"], ["TAB"])
                    V(lambda e, g=g: e.tensor_scalar(out=targ, in0=iota1, scalar1=th1[:, g:g + 1], scalar2=0.5 * PI, op0=ALU.mult, op1=ALU.add),
                      r=("cst", "th1", "targ", "TAB"), w=("targ",))
                    sin_of(COS[:, g_, :], targ, TB, ["targ"], ["TAB"])
                    V(lambda e, g_=g_, g=g: e.tensor_scalar(out=RT[:, g_, :], in0=iota1, scalar1=0.0, scalar2=r1[:, g:g + 1],
                                                            op0=ALU.mult, op1=ALU.add),
                      eng="pool", r=("cst", "r1", "TAB"), w=("TAB",))
                yi = c8 % 2
                for tb in range(NB):
                    tsl = slice(tb * TB, (tb + 1) * TB)
                    psy, psyk = PSY[tb % 2]
                    for g_ in range(8):
                        b = it % 2
                        it += 1
                        p1, p1k = PS1[b]
                        p2, p2k = PS2[b]
                        kb = bk[b]
                        S.add("pe", lambda e, p1=p1, g_=g_, tsl=tsl, ui=ui: e.matmul(p1[:], lhsT=ZB[:, g_, :], rhs=u[ui][:, tsl], start=True, stop=True),
                              reads=["Z", uk[ui]], writes=[p1k])
                        S.add("pe", lambda e, p2=p2, g_=g_, tsl=tsl, ui=ui: e.matmul(p2[:], lhsT=ZB2[:, g_, :], rhs=u[ui][:, tsl], start=True, stop=True),
                              reads=["Z", uk[ui]], writes=[p2k])
                        S.add("dve", lambda e, b=b, p1=p1, g_=g_: e.tensor_tensor(out=t1[b], in0=p1[:], in1=COS[:, g_, :], op=ALU.mult),
                              reads=[p1k, "TAB"], writes=[kb[0]])
                        S.add("dve", lambda e, b=b, p2=p2, g_=g_: e.tensor_tensor(out=t2[b], in0=p2[:], in1=SIN[:, g_, :], op=ALU.mult),
                              reads=[p2k, "TAB"], writes=[kb[1]])
                        S.add("pool", lambda e, b=b: e.tensor_tensor(out=vt[b], in0=t1[b], in1=t2[b], op=ALU.add),
                              reads=[kb[0], kb[1]], writes=[kb[2]])
                        init = 0.0 if tb == 0 else st8[:, g_:g_ + 1]
                        S.add("dve", lambda e, b=b, g_=g_, init=init: e.tensor_tensor_scan(out=vv[b], data0=RT[:, g_, :], data1=vt[b], initial=init,
                                                                                         op0=ALU.mult, op1=ALU.add),
                              reads=[kb[2], "TAB", ("st8", g_)], writes=[kb[3]])
                        if tb < NB - 1:
                            S.add("pe", lambda e, b=b, g_=g_: e.matmul(pst32[:, g_:g_ + 1], lhsT=ROT[:, g_, :], rhs=vv[b][:, TB - 1:TB], start=True, stop=True),
                                  reads=[kb[3], "ROT"], writes=[("pst", g_)])
                            S.add("act", lambda e, g_=g_: e.activation(out=st8[:, g_:g_ + 1], in_=pst32[:, g_:g_ + 1], func=AF.Copy),
                                  reads=[("pst", g_)], writes=[("st8", g_)])
                        S.add("pool", lambda e, b=b, g_=g_: e.tensor_tensor(out=w1[b], in0=vv[b], in1=COS[:, g_, :], op=ALU.mult),
                              reads=[kb[3], "TAB"], writes=[kb[4]])
                        S.add("pool", lambda e, b=b, g_=g_: e.tensor_tensor(out=w2[b], in0=vv[b], in1=SIN[:, g_, :], op=ALU.mult),
                              reads=[kb[3], "TAB"], writes=[kb[5]])
                        S.add("pe", lambda e, b=b, g_=g_, psy=psy: e.matmul(psy[:], lhsT=ZCA[:, g_, :], rhs=w1[b], start=(g_ == 0), stop=False),
                              reads=[kb[4], "ZC"], writes=[psyk])
                        S.add("pe", lambda e, b=b, g_=g_, psy=psy: e.matmul(psy[:], lhsT=ZCB[:, g_, :], rhs=w2[b], start=False, stop=(g_ == 7)),
                              reads=[kb[5], "ZC"], writes=[psyk])
                    S.add("dve", lambda e, psy=psy, tsl=tsl, c8=c8, ui=ui: e.scalar_tensor_tensor(out=yy, in0=u[ui][:, tsl], scalar=ddc[:, c8:c8 + 1], in1=psy[:],
                                                                                         op0=ALU.mult, op1=ALU.add),
                          reads=[psyk, uk[ui], "ddc", "yy"], writes=["yy"])
                    S.add("act", lambda e: e.activation(out=gsq, in_=yy, func=AF.Square), reads=["yy", "gsq"], writes=["gsq"])
                    S.add("dve", lambda e: e.tensor_scalar(out=gsq, in0=gsq, scalar1=0.044715, scalar2=1.0, op0=ALU.mult, op1=ALU.add),
                          reads=["gsq"], writes=["gsq"])
                    S.add("dve", lambda e: e.tensor_tensor(out=gin, in0=gsq, in1=yy, op=ALU.mult), reads=["gsq", "yy", "gin"], writes=["gin"])
                    S.add("act", lambda e: e.activation(out=gin, in_=gin, func=AF.Sigmoid, scale=1.5957691216057308), reads=["gin"], writes=["gin"])
                    S.add("pool", lambda e, tsl=tsl, yi=yi: e.tensor_tensor(out=yb[yi][:, tsl], in0=yy, in1=gin, op=ALU.mult),
                          reads=["gin", "yy"], writes=[ybk[yi]])
                S.add("sp", lambda e, c8=c8, yi=yi: e.dma_start(out=yT[c8], in_=yb[yi]), reads=[ybk[yi]], writes=[("yT", c8)], dma_key=ybk[yi] + "st")
            phase_end()

        def hgrn_phase(jh, l):
            phase_begin()
            CM = A16(T)
            S.add("pool", lambda e: e.dma_start(out=CM, in_=cmask_d, max_dma_last_dim=8192), writes=["CM"], dma_key="CM")
            qr = A32(T)
            fr = A32(T)
            vr = A32(T)
            lf = A32(T)
            cum = A32(T)
            kk = A32(T)
            qt = A16(T)
            kt = A16(T)
            vb = A16(T)
            osb = A32(T)
            ecl = A32(NCK)
            ecm = A32(NCK + 1)
            ecd = A32(NCK)
            PTs = [A16(CH) for _ in range(2)]
            vtok = [A16(128) for _ in range(2)]
            ktok = [A16(128) for _ in range(2)]
            Sst = A32(128)
            Sbf = A16(128)
            stmp = A32(128)
            osq = [A16(512) for _ in range(2)]
            sdh = A32(512)
            rsh = A32(512)
            gn = A32(KC)
            S.add("sp", lambda e: e.dma_start(out=gn, in_=hg_gn[jh]), writes=["gn"], dma_key="gn")
            cum3 = cum.rearrange("p (c t) -> p c t", t=CH)
            lf3 = lf.rearrange("p (c t) -> p c t", t=CH)
            for hd in range(KC):
                lo = l * KC + hd
                H = "hg"
                S.add("sp", lambda e, hd=hd: e.dma_start(out=qr, in_=qfv[hd]), reads=[("qfv", hd)], writes=["qr"], dma_key="qr")
                S.add("sp", lambda e, hd=hd: e.dma_start(out=fr, in_=qfv[KC + hd]), reads=[("qfv", KC + hd)], writes=["fr"], dma_key="fr")
                S.add("sp", lambda e, hd=hd: e.dma_start(out=vr, in_=qfv[2 * KC + hd]), reads=[("qfv", 2 * KC + hd)], writes=["vr"], dma_key="vr")
                S.add("act", lambda e: e.activation(out=qr, in_=qr, func=AF.Silu), reads=["qr"], writes=["qr"])
                S.add("act", lambda e: e.activation(out=fr, in_=fr, func=AF.Sigmoid), reads=["fr"], writes=["fr"])
                S.add("dve", lambda e, lo=lo: e.tensor_scalar(out=fr, in0=fr, scalar1=omv[:, lo:lo + 1], scalar2=lbv[:, lo:lo + 1], op0=ALU.mult, op1=ALU.add),
                      reads=["fr", "omv", "lbv"], writes=["fr"])
                S.add("act", lambda e: e.activation(out=lf, in_=fr, func=AF.Ln), reads=["fr", "lf"], writes=["lf"])
                S.add("dve", lambda e: e.tensor_scalar(out=kk, in0=fr, scalar1=-1.0, scalar2=1.0, op0=ALU.mult, op1=ALU.add),
                      reads=["fr", "kk"], writes=["kk"])
                S.add("dve", lambda e: e.tensor_tensor_scan(out=cum, data0=CM, data1=lf, initial=0.0, op0=ALU.mult, op1=ALU.add),
                      reads=["CM", "lf", "cum"], writes=["cum"])
                cmid = cum3[:, :, CH // 2 - 1]
                clast = cum3[:, :, CH - 1]
                S.add("act", lambda e: e.activation(out=ecl, in_=clast, func=AF.Exp), reads=["cum", "ecl"], writes=["ecl"])
                S.add("act", lambda e: e.activation(out=ecm[:, 0:NCK], in_=cmid, func=AF.Exp), reads=["cum", "ecm"], writes=["ecm"])
                S.add("dve", lambda e: e.tensor_tensor(out=ecd, in0=clast, in1=cmid, op=ALU.subtract), reads=["cum", "ecd"], writes=["ecd"])
                S.add("act", lambda e: e.activation(out=ecd, in_=ecd, func=AF.Exp), reads=["ecd"], writes=["ecd"])
                S.add("dve", lambda e: e.tensor_tensor(out=lf3, in0=cum3, in1=cum3[:, :, CH // 2 - 1:CH // 2].to_broadcast([128, NCK, CH]),
                                                       op=ALU.subtract),
                      reads=["cum", "lf"], writes=["lf"])
                S.add("act", lambda e: e.activation(out=cum, in_=lf, func=AF.Exp), reads=["lf", "cum", "ecl", "ecm", "ecd"], writes=["cum"])
                S.add("act", lambda e: e.activation(out=lf, in_=lf, func=AF.Exp, scale=-1.0), reads=["lf", "cum"], writes=["lf"])
                S.add("dve", lambda e: e.tensor_tensor(out=qt, in0=qr, in1=cum, op=ALU.mult), reads=["qr", "cum", "qt"], writes=["qt"])
                S.add("pool", lambda e: e.tensor_tensor(out=kt, in0=kk, in1=lf, op=ALU.mult), reads=["kk", "lf", "kt"], writes=["kt"])
                S.add("act", lambda e: e.activation(out=vb, in_=vr, func=AF.Copy), reads=["vr", "vb"], writes=["vb"])
                for ch in range(NCK):
                    csl = slice(ch * CH, (ch + 1) * CH)
                    b = ch % 2
                    psS, psSk = PSF[b], "psf%d" % b
                    psO, psOk = PSF[2 + b], "psf%d" % (2 + b)
                    psT, psTk = PSF[4], "psf4"
                    pbv = PSB[0][0:CH, b * 128:(b + 1) * 128]
                    pbk = PSB[1][0:CH, b * 128:(b + 1) * 128]
                    S.add("pe", lambda e, psS=psS, csl=csl: e.matmul(psS[0:CH, 0:CH], lhsT=kt[:, csl], rhs=qt[:, csl], start=True, stop=True),
                          reads=["kt", "qt"], writes=[psSk])
                    S.add("dve", lambda e, psS=psS, b=b: e.tensor_tensor(out=PTs[b][0:CH, :], in0=psS[0:CH, 0:CH], in1=maskT[0:CH, 0:CH], op=ALU.mult),
                          reads=[psSk, "cst", ("PT", b)], writes=[("PT", b)])
                    S.add("pe", lambda e, pbv=pbv, csl=csl: e.transpose(pbv, vb[:, csl], identb), reads=["vb", "identb"], writes=[("pbv", b)])
                    S.add("act", lambda e, pbv=pbv, b=b: e.activation(out=vtok[b][0:CH, :], in_=pbv, func=AF.Copy), reads=[("pbv", b), ("vtok", b)], writes=[("vtok", b)])
                    S.add("pe", lambda e, pbk=pbk, csl=csl: e.transpose(pbk, kt[:, csl], identb), reads=["kt", "identb"], writes=[("pbk", b)])
                    S.add("act", lambda e, pbk=pbk, b=b: e.activation(out=ktok[b][0:CH, :], in_=pbk, func=AF.Copy), reads=[("pbk", b), ("ktok", b)], writes=[("ktok", b)])
                    S.add("pe", lambda e, psO=psO, b=b, ch=ch: e.matmul(psO[:, 0:CH], lhsT=vtok[b][0:CH, :], rhs=PTs[b][0:CH, :], start=True, stop=(ch == 0)),
                          reads=[("vtok", b), ("PT", b)], writes=[psOk])
                    if ch > 0:
                        S.add("pe", lambda e, psO=psO, csl=csl: e.matmul(psO[:, 0:CH], lhsT=Sbf, rhs=qt[:, csl], start=False, stop=True),
                              reads=["Sbf", "qt"], writes=[psOk])
                    S.add("act", lambda e, psO=psO, csl=csl: e.activation(out=osb[:, csl], in_=psO[:, 0:CH], func=AF.Copy),
                          reads=[psOk, "osb"], writes=["osb"])
                    if ch < NCK - 1:
                        S.add("pe", lambda e, psT=psT, b=b: e.matmul(psT[:, 0:128], lhsT=ktok[b][0:CH, :], rhs=vtok[b][0:CH, :], start=True, stop=True),
                              reads=[("ktok", b), ("vtok", b)], writes=[psTk])
                        if ch == 0:
                            S.add("dve", lambda e, psT=psT, ch=ch: e.tensor_scalar(out=Sst, in0=psT[:, 0:128], scalar1=ecd[:, ch:ch + 1], scalar2=None, op0=ALU.mult),
                                  reads=[psTk, "ecd", "Sst"], writes=["Sst"])
                        else:
                            S.add("dve", lambda e, psT=psT, ch=ch: e.tensor_scalar(out=stmp, in0=psT[:, 0:128], scalar1=ecd[:, ch:ch + 1], scalar2=None, op0=ALU.mult),
                                  reads=[psTk, "ecd", "stmp"], writes=["stmp"])
                            S.add("dve", lambda e, ch=ch: e.scalar_tensor_tensor(out=Sst, in0=Sst, scalar=ecl[:, ch:ch + 1], in1=stmp, op0=ALU.mult, op1=ALU.add),
                                  reads=["stmp", "ecl", "Sst"], writes=["Sst"])
                        S.add("dve", lambda e, ch=ch: e.tensor_scalar(out=Sbf, in0=Sst, scalar1=ecm[:, ch + 1:ch + 2], scalar2=None, op0=ALU.mult),
                              reads=["Sst", "ecm", "Sbf"], writes=["Sbf"])
                for tb in range(T // 512):
                    tsl = slice(tb * 512, (tb + 1) * 512)
                    b = tb % 2
                    S.add("act", lambda e, b=b, tsl=tsl: e.activation(out=osq[b], in_=osb[:, tsl], func=AF.Square), reads=["osb", ("osq", b)], writes=[("osq", b)])
                    S.add("pe", lambda e, b=b: e.matmul(PSF[5][:], lhsT=onesb, rhs=osq[b], start=True, stop=True), reads=[("osq", b), "onesb"], writes=["psf5"])
                    S.add("act", lambda e: e.activation(out=sdh, in_=PSF[5][:], func=AF.Sqrt, scale=1.0 / 128, bias=EPS), reads=["psf5", "sdh"], writes=["sdh"])
                    S.add("dve", lambda e: e.reciprocal(out=rsh, in_=sdh), reads=["sdh", "rsh"], writes=["rsh"])
                    S.add("dve", lambda e, tsl=tsl, hd=hd: e.scalar_tensor_tensor(out=osb[:, tsl], in0=osb[:, tsl], scalar=gn[:, hd:hd + 1], in1=rsh,
                                                                                 op0=ALU.mult, op1=ALU.mult),
                          reads=["rsh", "gn", "osb"], writes=["osb"])
                S.add("sp", lambda e, hd=hd: e.dma_start(out=onT[hd], in_=osb), reads=["osb"], writes=[("onT", hd)], dma_key="osbst")
            phase_end()

        for l in range(DEPTH if KSTOP < 0 else KSTOP):
            j = l // 2
            is_s5 = (l % 2 == 0)
            phase_begin()
            Dn = Dense(need_g=False)
            for tt in range(NT):
                rmsnorm_to_aT(Dn, hT, "hT", tt, l, 0)
                if is_s5:
                    for kc in range(KC):
                        S.add("sp", lambda e, kc=kc, tt=tt, Dn=Dn: e.dma_start(out=uT[kc][:, tslice(tt)], in_=Dn.aT[kc]),
                              reads=[(Dn.aTk, kc)], writes=[("uT", kc)], dma_key=Dn.aTk + "st%d" % (kc % 4))
                else:
                    def win_cons(n, outs, tt=tt):
                        (ps, pk), = outs
                        ms, mk = Dn.mslot()
                        S.add("act", lambda e: e.activation(out=ms, in_=ps[:], func=AF.Copy), reads=[pk], writes=[mk])
                        S.add("sp", lambda e: e.dma_start(out=qfv[n][:, tslice(tt)], in_=ms), reads=[mk], writes=[("qfv", n)], dma_key=mk + "st")
                    linear(Dn, hg_win[j], 4 * KC, KC, False, win_cons)
            phase_end()
            if is_s5:
                s5_phase(j)
            else:
                hgrn_phase(j, l)
            phase_begin()
            Dn = Dense(need_g=True)
            for tt in range(NT):
                if is_s5:
                    for kc in range(KC):
                        S.add("sp", lambda e, kc=kc, tt=tt, Dn=Dn: e.dma_start(out=Dn.aT[kc], in_=yT[kc][:, tslice(tt)]),
                              reads=[("yT", kc)], writes=[(Dn.aTk, kc)], dma_key=Dn.aTk + "ld%d" % (kc % 4))

                    def glu_cons(n, outs):
                        (pv, pvk), (pg, pgk) = outs
                        t, tk = Dn.tslot()
                        ms, mk = Dn.mslot()
                        S.add("act", lambda e: e.activation(out=t, in_=pg[:], func=AF.Sigmoid), reads=[pgk], writes=[tk])
                        S.add("dve", lambda e: e.tensor_tensor(out=ms, in0=pv[:], in1=t, op=ALU.mult), reads=[pvk, tk], writes=[mk])
                        out_chunk(Dn, n, KC - 1, ms, mk, already_sbuf=True)
                    linear(Dn, s5_wg[j], KC, KC, True, glu_cons)
                else:
                    for kc in range(KC):
                        os_, ok = load_chunk(Dn, onT[kc][:, tslice(tt)], ("onT", kc))
                        gs, gk = load_chunk(Dn, qfv[3 * KC + kc][:, tslice(tt)], ("qfv", 3 * KC + kc))
                        S.add("act", lambda e, gs=gs: e.activation(out=gs, in_=gs, func=AF.Silu), reads=[gk], writes=[gk])
                        S.add("dve", lambda e, os_=os_, gs=gs, kc=kc, Dn=Dn: e.tensor_tensor(out=Dn.aT[kc], in0=os_, in1=gs, op=ALU.mult),
                              reads=[ok, gk], writes=[(Dn.aTk, kc)])

                    def wo_cons(n, outs):
                        (ps, pk), = outs
                        out_chunk(Dn, n, KC - 1, ps[:], pk)
                    linear(Dn, hg_wout[j], KC, KC, False, wo_cons)
                resid_update(Dn, tt, l, 1, hT, "hT")
                ffn_tile(Dn, tt, l, l == DEPTH - 1)
            fkeys = [k for k in S.dma_count if k.endswith("st") and k.startswith("hs")] if l == DEPTH - 1 else ()
            phase_end(final=fkeys)
        build_program.stats = (len(S.all_ops), len(S.dma_count), {e: len(S.ops[e]) for e in ENGS})
    return nc


def tile_w(W):
    K, N = W.shape
    return np.ascontiguousarray(W.reshape(K // 128, 128, N // 128, 128).transpose(2, 1, 0, 3))


def host_consts(T):
    c = np.zeros((128, 5 * 128 + 8 + 1 + 512), np.float32)
    c[:, 0:128] = np.eye(128)
    sw = np.zeros((128, 128), np.float32)
    for p in range(64):
        sw[p, 64 + p] = 1
        sw[64 + p, p] = 1
    c[:, 128:256] = sw
    s = np.arange(128)[:, None]
    t = np.arange(128)[None, :]
    c[:, 256:384] = (s <= t)
    c[:, 384:512] = 1.0
    for g in range(8):
        c[g * 16:(g + 1) * 16, 640 + g] = 1.0
    c[0:64, 648] = 1.0
    c[64:128, 648] = -1.0
    c[:, 649:649 + 512] = np.arange(1, 513, dtype=np.float32)[None, :]
    cm = np.ones((128, T), np.float32)
    cm[:, 0::HG_CH] = 0.0
    return c, cm


def prep_shared(inp, cfg):
    D, KC, G, DEPTH, NS5, NHG = cfg.D, cfg.KC, cfg.G, cfg.DEPTH, cfg.NS5, cfg.NHG
    f = lambda a: np.ascontiguousarray(np.asarray(a, dtype=np.float32))
    m = {}
    ng = f(inp["norm_gains"])
    m["gains"] = f(ng.reshape(DEPTH, 4, KC, 128).transpose(3, 0, 1, 2).reshape(128, DEPTH * 4 * KC))
    c, cm = host_consts(cfg.T)
    m["cst"], m["cmask"] = c, cm
    lbl = f(inp["hgrn_lb_logits"])
    m["lbl"] = f(lbl.reshape(DEPTH, KC, 128).transpose(2, 0, 1).reshape(128, DEPTH * KC))
    are, aim, ldt = f(inp["s5_a_re"]), f(inp["s5_a_im"]), f(inp["s5_log_dt"])
    a1 = np.zeros((NS5, 3, 128, G), np.float32)
    a1[:, 0] = np.concatenate([are.transpose(0, 2, 1)] * 2, axis=1)
    a1[:, 1] = np.concatenate([aim.transpose(0, 2, 1)] * 2, axis=1)
    a1[:, 2] = np.broadcast_to(ldt[:, None, :], (NS5, 128, G))
    m["s5_a1"] = a1
    a2 = np.zeros((NS5, KC, 3, 128, 64), np.float32)
    rep = lambda z: np.broadcast_to(z.reshape(NS5, KC, 8, 1, 64), (NS5, KC, 8, 16, 64)).reshape(NS5, KC, 128, 64)
    a2[:, :, 0] = rep(are)
    a2[:, :, 1] = rep(aim)
    a2[:, :, 2] = rep(np.broadcast_to(ldt[:, :, None], (NS5, G, 64)))
    m["s5_a2"] = a2
    bre, bim = f(inp["s5_b_re"]), f(inp["s5_b_im"])
    bt = lambda z: z.reshape(NS5, KC, 8, 64, 16).transpose(0, 1, 2, 4, 3).reshape(NS5, KC, 128, 64)
    m["s5_b2"] = f(np.stack([bt(bre), bt(bim)], axis=2))
    cre, cim = f(inp["s5_c_re"]), f(inp["s5_c_im"])
    ct = lambda z: z.reshape(NS5, KC, 8, 16, 64).transpose(0, 1, 4, 2, 3).reshape(NS5, KC, 64, 128)
    c0 = np.concatenate([ct(cre), ct(cim)], axis=2)
    c1 = np.concatenate([ct(cim), ct(cre)], axis=2)
    m["s5_c2"] = f(np.stack([c0, c1], axis=2))
    m["s5_dd"] = f(f(inp["s5_d"]).reshape(NS5, KC, 128).transpose(0, 2, 1))
    wg = f(inp["s5_w_glu"])
    m["s5_wg"] = f(np.stack([np.stack([tile_w(wg[j][:, :D]), tile_w(wg[j][:, D:])], axis=3) for j in range(NS5)]))
    m["hg_win"] = f(np.stack([tile_w(f(inp["hgrn_w_in"][j])) for j in range(NHG)]))
    m["hg_wout"] = f(np.stack([tile_w(f(inp["hgrn_w_out"][j])) for j in range(NHG)]))
    m["hg_gn"] = f(f(inp["hgrn_g_norm"]).reshape(NHG, KC, 128).transpose(0, 2, 1))
    gu = inp["ffn_w_gate_up"]
    DFF = cfg.DFF
    m["ff_gu"] = f(np.stack([np.stack([tile_w(f(gu[l][:, :DFF])), tile_w(f(gu[l][:, DFF:]))], axis=3) for l in range(DEPTH)]))
    m["ff_dn"] = f(np.stack([tile_w(f(inp["ffn_w_down"][l])) for l in range(DEPTH)]))
    return m


_CACHE = {}


def run(inp, cfg):
    x = np.asarray(inp["x"], dtype=np.float32)
    B = x.shape[0]
    shared = prep_shared(inp, cfg)
    in_maps = []
    for b in range(B):
        m = dict(shared)
        m["xT"] = np.ascontiguousarray(x[b].T.reshape(cfg.KC, 128, cfg.T))
        in_maps.append(m)
    kk = (cfg.D, cfg.T, cfg.DFF, cfg.DEPTH)
    if kk not in _CACHE:
        _CACHE[kk] = build_program(cfg)
    nc = _CACHE[kk]
    res = run_bass_kernel_spmd(nc, in_maps, core_ids=list(range(B)))
    out = np.stack([res.results[b]["outT"].reshape(cfg.D, cfg.T).T for b in range(B)], axis=0)
    if DEBUG:
        run.dbg = res.results
    return np.ascontiguousarray(out.astype(np.float32))


def kernel(**inputs):
    cfg = Cfg(D=4096, T=4096, DFF=11008, DEPTH=4)
    return run(inputs, cfg)
```

```python
import contextlib
import math
import numpy as np
import concourse.bass as bass
import concourse.mybir as mybir
from concourse.bass_utils import run_bass_kernel_spmd

F32 = mybir.dt.float32
BF16 = mybir.dt.bfloat16
AF = mybir.ActivationFunctionType
ALU = mybir.AluOpType
ENGS = ("pe", "dve", "act", "pool", "sp")
PI = math.pi
import os
NOWAIT = bool(int(os.environ.get('NOWAIT', '0')))
ONLY = os.environ.get('ONLY', '')
DEBUG = bool(int(os.environ.get('KDEBUG', '0')))
KSTOP = int(os.environ.get('KSTOP', '-1'))
EPS = 1e-6
HG_CH = 64


class Op:
    __slots__ = ("eng", "fn", "dma_key", "deps", "milestone", "sig_val", "is_dma")

    def __init__(self, eng, fn, dma_key):
        self.eng = eng
        self.fn = fn
        self.dma_key = dma_key
        self.is_dma = dma_key is not None
        self.deps = []
        self.milestone = False
        self.sig_val = None


class Sched:
    def __init__(self, nc):
        self.nc = nc
        self.ops = {e: [] for e in ENGS}
        self.last_writer = {}
        self.readers = {}
        self.dma_count = {}
        self.last_dma = {}
        self.all_ops = []
        self.pending = {e: [] for e in ENGS}

    def barrier(self):
        deps = []
        for e in ENGS:
            if self.ops[e]:
                deps.append(self.ops[e][-1])
        deps.extend(self.last_dma.values())
        for e in ENGS:
            self.pending[e] = list(deps)
        self.last_writer = {}
        self.readers = {}

    def add(self, eng, fn, reads=(), writes=(), dma_key=None):
        op = Op(eng, fn, dma_key)
        deps = list(self.pending[eng])
        self.pending[eng] = []
        for r in reads:
            lw = self.last_writer.get(r)
            if lw is not None:
                deps.append(lw)
        for w in writes:
            lw = self.last_writer.get(w)
            if lw is not None:
                deps.append(lw)
            deps.extend(self.readers.get(w, ()))
        if op.is_dma and dma_key in self.last_dma:
            deps.append(self.last_dma[dma_key])
        for r in reads:
            self.readers.setdefault(r, []).append(op)
        for w in writes:
            self.last_writer[w] = op
            self.readers[w] = []
        seen = set()
        for d in deps:
            if d is op or id(d) in seen:
                continue
            seen.add(id(d))
            if d.eng == "pe" and eng == "pe" and not d.is_dma and not op.is_dma:
                continue
            op.deps.append(d)
        if op.is_dma:
            c = self.dma_count.get(dma_key, 0) + 1
            self.dma_count[dma_key] = c
            op.sig_val = 16 * c
            self.last_dma[dma_key] = op
        self.ops[eng].append(op)
        self.all_ops.append(op)
        return op

    def setup(self, stack):
        self.stack = stack
        self.esem = {e: stack.enter_context(self.nc.semaphore("s_" + e)) for e in ENGS}
        self.dsem = {}
        self.counters = {e: 0 for e in ENGS}
        self.seen_e = {e: {x: 0 for x in ENGS} for e in ENGS}
        self.seen_d = {e: {} for e in ENGS}
        self.cur = {e: [] for e in ENGS}
        self.emitted = 0

    def flush(self, final_dma_keys=()):
        nc = self.nc
        new_ops = {e: self.ops[e][len(self.cur[e]):] for e in ENGS}
        for e in ENGS:
            for op in new_ops[e]:
                for d in op.deps:
                    if not d.is_dma:
                        d.milestone = True
            if new_ops[e]:
                last = new_ops[e][-1]
                if not last.is_dma:
                    last.milestone = True
        for e in ENGS:
            for op in new_ops[e]:
                if op.milestone and not op.is_dma:
                    self.counters[e] += 1
                    op.sig_val = self.counters[e]
                if op.is_dma and op.dma_key not in self.dsem:
                    self.dsem[op.dma_key] = self.stack.enter_context(nc.semaphore("d_%d" % len(self.dsem)))
        esem, dsem = self.esem, self.dsem
        with nc.Block() as block:
            def run(e, engobj):
                seen_e = self.seen_e[e]
                seen_d = self.seen_d[e]
                for op in new_ops[e]:
                    for d in op.deps:
                        if d.is_dma:
                            if seen_d.get(d.dma_key, 0) < d.sig_val:
                                engobj.wait_ge(dsem[d.dma_key], d.sig_val)
                                seen_d[d.dma_key] = d.sig_val
                        else:
                            if seen_e[d.eng] < d.sig_val:
                                engobj.wait_ge(esem[d.eng], d.sig_val)
                                seen_e[d.eng] = d.sig_val
                    ins = op.fn(engobj)
                    op.fn = None
                    if op.is_dma:
                        ins.then_inc(dsem[op.dma_key], 16)
                    elif op.milestone:
                        ins.then_inc(esem[e], 1)
                if e == "sp":
                    for k in final_dma_keys:
                        engobj.wait_ge(dsem[k], 16 * self.dma_count[k])

            @block.tensor
            def _(eng):
                run("pe", eng)

            @block.vector
            def _(eng):
                run("dve", eng)

            @block.scalar
            def _(eng):
                run("act", eng)

            @block.gpsimd
            def _(eng):
                run("pool", eng)

            @block.sync
            def _(eng):
                run("sp", eng)
        for e in ENGS:
            self.cur[e] = list(self.ops[e])
        self.barrier()


class Cfg:
    def __init__(self, D=4096, T=4096, DFF=11008, DEPTH=4):
        self.D, self.T, self.DFF, self.DEPTH = D, T, DFF, DEPTH
        self.KC = D // 128
        self.TT = 512
        self.NT = T // 512
        self.JF = DFF // 128
        self.G = D // 16
        self.NS5 = (DEPTH + 1) // 2
        self.NHG = DEPTH // 2
        self.ARENA = 48 * 1024


def build_program(cfg):
    D, T, KC, TT, NT, JF, G, DEPTH = cfg.D, cfg.T, cfg.KC, cfg.TT, cfg.NT, cfg.JF, cfg.G, cfg.DEPTH
    NS5, NHG = cfg.NS5, cfg.NHG
    CH = HG_CH
    NCK = T // CH
    nc = bass.Bass("TRN2", target_bir_lowering=False)

    def din(name, shape, dt=F32):
        return nc.dram_tensor(name, list(shape), dt, kind="ExternalInput").ap()

    def dscr(name, shape, dt=F32):
        if DEBUG:
            return nc.dram_tensor(name, list(shape), dt, kind="ExternalOutput").ap()
        return nc.dram_tensor(name, list(shape), dt).ap()

    xT = din("xT", [KC, 128, T])
    gains_d = din("gains", [128, DEPTH * 4 * KC])
    cst_d = din("cst", [128, 5 * 128 + 8 + 1 + 512])
    cmask_d = din("cmask", [128, T])
    lbl_d = din("lbl", [128, DEPTH * KC])
    s5_a1 = din("s5_a1", [NS5, 3, 128, G])
    s5_a2 = din("s5_a2", [NS5, KC, 3, 128, 64])
    s5_b2 = din("s5_b2", [NS5, KC, 2, 128, 64])
    s5_c2 = din("s5_c2", [NS5, KC, 2, 128, 128])
    s5_dd = din("s5_dd", [NS5, 128, KC])
    s5_wg = din("s5_wg", [NS5, KC, 128, KC, 2, 128])
    hg_win = din("hg_win", [NHG, 4 * KC, 128, KC, 128])
    hg_wout = din("hg_wout", [NHG, KC, 128, KC, 128])
    hg_gn = din("hg_gn", [NHG, 128, KC])
    ff_gu = din("ff_gu", [DEPTH, JF, 128, KC, 2, 128])
    ff_dn = din("ff_dn", [DEPTH, KC, 128, JF, 128])
    outT = nc.dram_tensor("outT", [KC, 128, T], F32, kind="ExternalOutput").ap()

    hT = dscr("hT", [KC, 128, T])
    uT = dscr("uT", [KC, 128, T], BF16)
    yT = dscr("yT", [KC, 128, T], BF16)
    qfv = dscr("qfv", [4 * KC, 128, T])
    onT = dscr("onT", [KC, 128, T])
    msc = dscr("msc", [KC, 128, TT])
    wc_gu = [nc.dram_tensor("wc_gu%d" % l, [JF, 128, KC * 2 * 128], BF16).ap() for l in range(DEPTH)]
    wc_dn = [nc.dram_tensor("wc_dn%d" % l, [KC, 128, JF * 128], BF16).ap() for l in range(DEPTH)]
    wc_glu = [nc.dram_tensor("wc_glu%d" % l, [KC, 128, KC * 2 * 128], BF16).ap() for l in range(NS5)]
    wc_win = [nc.dram_tensor("wc_win%d" % l, [4 * KC, 128, KC * 128], BF16).ap() for l in range(NHG)]
    wc_wout = [nc.dram_tensor("wc_wout%d" % l, [KC, 128, KC * 128], BF16).ap() for l in range(NHG)]

    with contextlib.ExitStack() as st:
        cst = st.enter_context(nc.sbuf_tensor("cstsb", [128, 5 * 128 + 8 + 1 + 512], F32))
        gains = st.enter_context(nc.sbuf_tensor("gainsb", [128, DEPTH * 4 * KC], F32))
        cbf = st.enter_context(nc.sbuf_tensor("cbf", [128, 2 * 128], BF16))
        lbt = st.enter_context(nc.sbuf_tensor("lbt", [128, 3 * DEPTH * KC], F32))
        PSF = [st.enter_context(nc.psum_tensor("psf%d" % i, [128, 512], F32)) for i in range(6)]
        PSB = [st.enter_context(nc.psum_tensor("psb%d" % i, [128, 1024], BF16)) for i in range(2)]

        S = Sched(nc)
        S.setup(st)
        uid = [0]
        ident = cst[:, 0:128]
        swapm = cst[:, 128:256]
        maskT = cst[:, 256:384]
        ones32 = cst[:, 384:512]
        mask8 = cst[:, 640:648]
        sgn = cst[:, 648:649]
        iota1 = cst[:, 649:649 + 512]
        identb = cbf[:, 0:128]
        onesb = cbf[:, 128:256]

        PH = [None]

        def phase_begin():
            PH[0] = contextlib.ExitStack()

        def phase_end(final=()):
            S.flush(final_dma_keys=final)
            PH[0].close()
            PH[0] = None

        def A32(n):
            uid[0] += 1
            return PH[0].enter_context(nc.sbuf_tensor("t%d" % uid[0], [128, n], F32))[:]

        def A16(n):
            uid[0] += 1
            return PH[0].enter_context(nc.sbuf_tensor("t%d" % uid[0], [128, n], BF16))[:]

        def AI32(n):
            uid[0] += 1
            return PH[0].enter_context(nc.sbuf_tensor("t%d" % uid[0], [128, n], mybir.dt.int32))[:]

        def key(prefix):
            uid[0] += 1
            return "%s%d" % (prefix, uid[0])

        S.add("sp", lambda e: e.dma_start(out=cst[:], in_=cst_d), writes=["cst"], dma_key="cst")
        S.add("sp", lambda e: e.dma_start(out=gains[:], in_=gains_d), writes=["gains"], dma_key="gains")
        S.add("sp", lambda e: e.dma_start(out=lbt[:, 0:DEPTH * KC], in_=lbl_d), writes=["lbl"], dma_key="lbl")
        S.add("dve", lambda e: e.tensor_copy(out=identb, in_=ident), reads=["cst"], writes=["identb"])
        S.add("dve", lambda e: e.tensor_copy(out=onesb, in_=ones32), reads=["cst"], writes=["onesb"])
        lbe = lbt[:, 0:DEPTH * KC]
        lbv = lbt[:, DEPTH * KC:2 * DEPTH * KC]
        omv = lbt[:, 2 * DEPTH * KC:3 * DEPTH * KC]
        S.add("act", lambda e: e.activation(out=lbe, in_=lbe, func=AF.Exp), reads=["lbl"], writes=["lbl"])
        def _lbsum(e):
            ins = e.tensor_copy(out=omv[:, 0:KC], in_=lbe[:, 0:KC])
            return ins
        S.add("dve", _lbsum, reads=["lbl"], writes=["om0"])
        for l in range(1, DEPTH):
            S.add("dve", lambda e, l=l: e.tensor_tensor(out=omv[:, 0:KC], in0=omv[:, 0:KC], in1=lbe[:, l * KC:(l + 1) * KC], op=ALU.add),
                  reads=["lbl", "om0"], writes=["om0"])
        S.add("dve", lambda e: e.reciprocal(out=omv[:, KC:2 * KC], in_=omv[:, 0:KC]), reads=["om0"], writes=["om1"])
        for l in range(DEPTH):
            S.add("dve", lambda e, l=l: e.tensor_tensor(out=lbe[:, l * KC:(l + 1) * KC], in0=lbe[:, l * KC:(l + 1) * KC],
                                                       in1=omv[:, KC:2 * KC], op=ALU.mult),
                  reads=["lbl", "om1"], writes=["lbl"])
        S.add("dve", lambda e: e.memset(lbv[:, 0:KC], 0.0), writes=["lbv"])
        for l in range(1, DEPTH):
            S.add("dve", lambda e, l=l: e.tensor_tensor(out=lbv[:, l * KC:(l + 1) * KC], in0=lbv[:, (l - 1) * KC:l * KC],
                                                       in1=lbe[:, l * KC:(l + 1) * KC], op=ALU.add),
                  reads=["lbl", "lbv"], writes=["lbv"])
        S.add("dve", lambda e: e.tensor_scalar(out=omv, in0=lbv, scalar1=-1.0, scalar2=1.0, op0=ALU.mult, op1=ALU.add),
              reads=["lbv", "om1"], writes=["omv"])

        def gcol(l, i, kc):
            o = (l * 4 + i) * KC + kc
            return gains[:, o:o + 1]

        def tslice(tt):
            return slice(tt * TT, (tt + 1) * TT)

        class Dense:
            def __init__(self, need_g):
                self.aT = [A16(TT) for _ in range(KC)]
                self.gT = [A16(TT) for _ in range(JF)] if need_g else None
                wsz = max(KC * 2 * 128, JF * 128)
                self.NW = 2
                self.w = [A16(wsz) for _ in range(self.NW)]
                self.wk = ["w%d" % i for i in range(self.NW)]
                self.wi = 0
                self.NH = 4
                self.hs = [A32(TT) for _ in range(self.NH)]
                self.hk = ["hs%d" % i for i in range(self.NH)]
                self.hi = 0
                self.ms = [A32(TT) for _ in range(self.NH)]
                self.mk = ["ms%d" % i for i in range(self.NH)]
                self.mi = 0
                self.sq = [A16(TT) for _ in range(2)]
                self.sqk = ["sq%d" % i for i in range(2)]
                self.sqi = 0
                self.sd = A32(TT)
                self.rstd = A32(TT)
                self.tmp = [A32(TT) for _ in range(2)]
                self.tk = ["tmp%d" % i for i in range(2)]
                self.ti = 0
                self.psi = 0
                self.aTk = "aT"
                self.gTk = "gT"
                self.ssk = "ss"
                self.rk = "rstd"

            def hslot(self):
                i = self.hi % self.NH
                self.hi += 1
                return self.hs[i], self.hk[i]

            def mslot(self):
                i = self.mi % self.NH
                self.mi += 1
                return self.ms[i], self.mk[i]

            def sqslot(self):
                i = self.sqi % 2
                self.sqi += 1
                return self.sq[i], self.sqk[i]

            def tslot(self):
                i = self.ti % 2
                self.ti += 1
                return self.tmp[i], self.tk[i]

            def wslot(self):
                i = self.wi % self.NW
                self.wi += 1
                return self.w[i], self.wk[i]

            def psum(self):
                i = self.psi % 4
                self.psi += 1
                return PSF[i], "psf%d" % i

        PS_SS, PS_SSK = PSF[4], "psf4"

        def load_chunk(Dn, src_ap, src_key):
            slot, k = Dn.hslot()
            S.add("sp", lambda e: e.dma_start(out=slot, in_=src_ap), reads=[src_key], writes=[k], dma_key=k)
            return slot, k

        def ss_accum(Dn, src, srck, first, last):
            sq, sqk = Dn.sqslot()
            S.add("act", lambda e: e.activation(out=sq, in_=src, func=AF.Square), reads=[srck], writes=[sqk])
            S.add("pe", lambda e: e.matmul(PS_SS[:], lhsT=onesb, rhs=sq, start=first, stop=last),
                  reads=[sqk, "onesb"], writes=[PS_SSK])

        def make_rstd(Dn, n_feat):
            S.add("act", lambda e: e.activation(out=Dn.sd, in_=PS_SS[:], func=AF.Sqrt, scale=1.0 / n_feat, bias=EPS),
                  reads=[PS_SSK], writes=[Dn.rk + "sd"])
            S.add("dve", lambda e: e.reciprocal(out=Dn.rstd, in_=Dn.sd), reads=[Dn.rk + "sd"], writes=[Dn.rk])

        def rmsnorm_to_aT(Dn, src, srcname, tt, l, gi):
            for kc in range(KC):
                slot, k = load_chunk(Dn, src[kc][:, tslice(tt)], (srcname, kc))
                ss_accum(Dn, slot, k, kc == 0, kc == KC - 1)
            make_rstd(Dn, D)
            for kc in range(KC):
                slot, k = load_chunk(Dn, src[kc][:, tslice(tt)], (srcname, kc))
                S.add("dve", lambda e, slot=slot, kc=kc: e.scalar_tensor_tensor(
                    out=Dn.aT[kc], in0=slot, scalar=gcol(l, gi, kc), in1=Dn.rstd, op0=ALU.mult, op1=ALU.mult),
                    reads=[k, Dn.rk, "gains"], writes=[(Dn.aTk, kc)])

        def linear(Dn, Wd, nch, kcin, pair, consumer, rhs_of=None, rkeys=None, cache=None, cname=None, first=True):
            if rhs_of is None:
                rhs_of = lambda kc: Dn.aT[kc]
                rkeys = lambda kc: (Dn.aTk, kc)
            for n in range(nch):
                w, wk = Dn.wslot()
                npair = 2 if pair else 1
                wv = w[:, 0:kcin * npair * 128]
                src = Wd[n]
                if pair:
                    src = src.rearrange("p k a c -> p (k a c)")
                    wv3 = wv.rearrange("p (k a c) -> p k a c", a=2, c=128)
                else:
                    src = src.rearrange("p k c -> p (k c)")
                    wv3 = wv.rearrange("p (k c) -> p k c", c=128)
                if cache is None or first:
                    S.add("pool", lambda e, wv=wv, src=src: e.dma_start(out=wv, in_=src, max_dma_last_dim=8192),
                          writes=[wk], dma_key=wk)
                    if cache is not None:
                        S.add("sp", lambda e, wv=wv, n=n: e.dma_start(out=cache[n], in_=wv), reads=[wk], writes=[(cname, n)],
                              dma_key=wk + "st")
                else:
                    S.add("pool", lambda e, wv=wv, n=n: e.dma_start(out=wv, in_=cache[n]), reads=[(cname, n)], writes=[wk],
                          dma_key=wk)
                outs = []
                for a in range(npair):
                    ps, pk = Dn.psum()
                    for kc in range(kcin):
                        lhsT = wv3[:, kc, a, :] if pair else wv3[:, kc, :]
                        S.add("pe", lambda e, ps=ps, lhsT=lhsT, kc=kc: e.matmul(ps[:], lhsT=lhsT, rhs=rhs_of(kc),
                                                                               start=(kc == 0), stop=(kc == kcin - 1)),
                              reads=[wk, rkeys(kc)], writes=[pk])
                    outs.append((ps, pk))
                consumer(n, outs)

        def out_chunk(Dn, n, nlast, src, srck, already_sbuf=False):
            if already_sbuf:
                ms, mk = src, srck
            else:
                ms, mk = Dn.mslot()
                S.add("act", lambda e: e.activation(out=ms, in_=src, func=AF.Copy), reads=[srck], writes=[mk])
            ss_accum(Dn, ms, mk, n == 0, n == nlast)
            S.add("sp", lambda e: e.dma_start(out=msc[n], in_=ms), reads=[mk], writes=[("msc", n)], dma_key=mk + "st")

        def resid_update(Dn, tt, l, gi, dst, dstname):
            make_rstd(Dn, D)
            for kc in range(KC):
                ms, mk = Dn.mslot()
                S.add("sp", lambda e, ms=ms, kc=kc: e.dma_start(out=ms, in_=msc[kc]), reads=[("msc", kc)], writes=[mk],
                      dma_key=mk)
                hs, hk = load_chunk(Dn, hT[kc][:, tslice(tt)], ("hT", kc))
                S.add("dve", lambda e, ms=ms, kc=kc: e.scalar_tensor_tensor(
                    out=ms, in0=ms, scalar=gcol(l, gi, kc), in1=Dn.rstd, op0=ALU.mult, op1=ALU.mult),
                    reads=[mk, Dn.rk, "gains"], writes=[mk])
                S.add("pool", lambda e, ms=ms, hs=hs: e.tensor_tensor(out=hs, in0=hs, in1=ms, op=ALU.add),
                      reads=[mk, hk], writes=[hk])
                S.add("sp", lambda e, hs=hs, kc=kc: e.dma_start(out=dst[kc][:, tslice(tt)], in_=hs), reads=[hk],
                      writes=[(dstname, kc)], dma_key=hk + "st")

        def ffn_tile(Dn, tt, l, last_layer):
            rmsnorm_to_aT(Dn, hT, "hT", tt, l, 2)

            def gu_cons(j, outs):
                (pg, pgk), (pu, puk) = outs
                t, tk = Dn.tslot()
                S.add("act", lambda e: e.activation(out=t, in_=pg[:], func=AF.Silu), reads=[pgk], writes=[tk])
                S.add("dve", lambda e: e.tensor_tensor(out=Dn.gT[j], in0=pu[:], in1=t, op=ALU.mult),
                      reads=[puk, tk], writes=[(Dn.gTk, j)])
            linear(Dn, ff_gu[l], JF, KC, True, gu_cons, cache=wc_gu[l], cname="wc_gu", first=(tt == 0))

            def dn_cons(m, outs):
                (pf, pfk), = outs
                out_chunk(Dn, m, KC - 1, pf[:], pfk)
            linear(Dn, ff_dn[l], KC, JF, False, dn_cons, rhs_of=lambda j: Dn.gT[j], rkeys=lambda j: (Dn.gTk, j),
                   cache=wc_dn[l], cname="wc_dn", first=(tt == 0))
            if last_layer:
                resid_update(Dn, tt, l, 3, outT, "outT")
            else:
                resid_update(Dn, tt, l, 3, hT, "hT")

        S.flush()
        phase_begin()
        cp = [A32(T) for _ in range(2)]
        cpk = ["cp0", "cp1"]
        for kc in range(KC):
            i = kc % 2
            S.add("sp", lambda e, i=i, kc=kc: e.dma_start(out=cp[i], in_=xT[kc]), writes=[cpk[i]], dma_key=cpk[i])
            S.add("sp", lambda e, i=i, kc=kc: e.dma_start(out=hT[kc], in_=cp[i]), reads=[cpk[i]], writes=[("hT", kc)],
                  dma_key=cpk[i] + "st")
        phase_end()

        def s5_phase(j):
            TB = 512
            NB = T // TB
            phase_begin()
            a1 = A32(3 * G).rearrange("p (a g) -> p a g", g=G)
            th1 = A32(G)
            r1 = A32(G)
            cosL = A32(G)
            ssinL = A32(G)
            t1g = A32(G)
            ddc = A32(KC)
            S.add("sp", lambda e: e.dma_start(out=a1, in_=s5_a1[j].rearrange("a p g -> p a g")), writes=["a1"], dma_key="a1")
            S.add("sp", lambda e: e.dma_start(out=ddc, in_=s5_dd[j]), writes=["ddc"], dma_key="ddc")
            S.add("act", lambda e: e.activation(out=a1[:, 2, :], in_=a1[:, 2, :], func=AF.Exp), reads=["a1"], writes=["a1"])
            S.add("dve", lambda e: e.tensor_tensor(out=th1, in0=a1[:, 1, :], in1=a1[:, 2, :], op=ALU.mult), reads=["a1"], writes=["th1"])
            S.add("dve", lambda e: e.scalar_tensor_tensor(out=r1, in0=a1[:, 0, :], scalar=-1e-4, in1=a1[:, 2, :], op0=ALU.min, op1=ALU.mult),
                  reads=["a1"], writes=["r1"])
            S.add("act", lambda e: e.activation(out=r1, in_=r1, func=AF.Exp), reads=["r1"], writes=["r1"])
            rr_f = A32(512)
            rr_i = AI32(512)

            def sin_of(out, x, n, rkeys, wkeys):
                f_ = rr_f[:, 0:n]
                i_ = rr_i[:, 0:n]
                RR = "rr"
                S.add("dve", lambda e: e.tensor_scalar(out=f_, in0=x, scalar1=1.0 / (2 * PI), scalar2=None, op0=ALU.mult),
                      reads=list(rkeys) + [RR], writes=[RR])
                S.add("dve", lambda e: e.tensor_copy(out=i_, in_=f_), reads=[RR], writes=[RR + "i"])
                S.add("dve", lambda e: e.tensor_copy(out=f_, in_=i_), reads=[RR + "i"], writes=[RR])
                S.add("dve", lambda e: e.scalar_tensor_tensor(out=x, in0=f_, scalar=-2 * PI, in1=x, op0=ALU.mult, op1=ALU.add),
                      reads=[RR] + list(rkeys), writes=list(rkeys))
                S.add("dve", lambda e: e.tensor_scalar(out=f_, in0=x, scalar1=PI, scalar2=2 * PI, op0=ALU.is_gt, op1=ALU.mult),
                      reads=list(rkeys) + [RR], writes=[RR])
                S.add("dve", lambda e: e.tensor_tensor(out=x, in0=x, in1=f_, op=ALU.subtract), reads=[RR] + list(rkeys), writes=list(rkeys))
                S.add("dve", lambda e: e.tensor_scalar(out=f_, in0=x, scalar1=-PI, scalar2=2 * PI, op0=ALU.is_lt, op1=ALU.mult),
                      reads=list(rkeys) + [RR], writes=[RR])
                S.add("dve", lambda e: e.tensor_tensor(out=x, in0=x, in1=f_, op=ALU.add), reads=[RR] + list(rkeys), writes=list(rkeys))
                S.add("act", lambda e: e.activation(out=out, in_=x, func=AF.Sin), reads=list(rkeys) + list(wkeys), writes=list(wkeys))

            S.add("dve", lambda e: e.tensor_scalar(out=t1g, in0=th1, scalar1=float(TB), scalar2=None, op0=ALU.mult),
                  reads=["th1"], writes=["t1g"])
            sin_of(ssinL, t1g, G, ["t1g"], ["ssinL"])
            S.add("dve", lambda e: e.tensor_scalar(out=ssinL, in0=ssinL, scalar1=sgn, scalar2=None, op0=ALU.mult),
                  reads=["ssinL", "cst"], writes=["ssinL"])
            S.add("dve", lambda e: e.tensor_scalar(out=t1g, in0=th1, scalar1=float(TB), scalar2=0.5 * PI, op0=ALU.mult, op1=ALU.add),
                  reads=["th1", "ssinL", "t1g"], writes=["t1g"])
            sin_of(cosL, t1g, G, ["t1g"], ["cosL"])

            u = [A16(T) for _ in range(2)]
            uk = ["u0", "u1"]
            yb = [A16(T) for _ in range(2)]
            ybk = ["yb0", "yb1"]
            a2 = A32(3 * 64).rearrange("p (a q) -> p a q", q=64)
            b2 = A32(2 * 64).rearrange("p (a q) -> p a q", q=64)
            c2 = A32(2 * 128).rearrange("p (a q) -> p a q", q=128)
            wk_ = [A32(64) for _ in range(12)]
            Bf = A32(128)
            Bf2 = A32(128)
            CA = A32(128)
            CB = A32(128)
            ZB = A16(8 * 128).rearrange("p (g c) -> p g c", c=128)
            ZB2 = A16(8 * 128).rearrange("p (g c) -> p g c", c=128)
            ZCA = A16(8 * 128).rearrange("p (g c) -> p g c", c=128)
            ZCB = A16(8 * 128).rearrange("p (g c) -> p g c", c=128)
            ROT = A32(8 * 128).rearrange("p (g c) -> p g c", c=128)
            COS = A32(8 * TB).rearrange("p (g t) -> p g t", t=TB)
            SIN = A32(8 * TB).rearrange("p (g t) -> p g t", t=TB)
            RT = A32(8 * TB).rearrange("p (g t) -> p g t", t=TB)
            targ = A32(TB)
            st8 = A32(8)
            t1 = [A32(TB) for _ in range(2)]
            t2 = [A32(TB) for _ in range(2)]
            vt = [A32(TB) for _ in range(2)]
            vv = [A32(TB) for _ in range(2)]
            w1 = [A16(TB) for _ in range(2)]
            w2 = [A16(TB) for _ in range(2)]
            yy = A32(TB)
            gsq = A32(TB)
            gin = A32(TB)
            bk = [["s5b%d_%d" % (a_, b_) for b_ in range(8)] for a_ in range(2)]
            PS1 = [(PSF[0], "psf0"), (PSF[1], "psf1")]
            PS2 = [(PSF[2], "psf2"), (PSF[3], "psf3")]
            PSY = [(PSF[4], "psf4"), (PSF[5], "psf5")]
            PST = (PSB[0], "psb0")
            pst32 = PSB[0][:].bitcast(F32)

            it = 0
            for c8 in range(KC):
                ui = c8 % 2
                S.add("sp", lambda e, ui=ui, c8=c8: e.dma_start(out=u[ui], in_=uT[c8]), reads=[("uT", c8)], writes=[uk[ui]],
                      dma_key=uk[ui])
                S.add("sp", lambda e, c8=c8: e.dma_start(out=a2, in_=s5_a2[j, c8].rearrange("a p q -> p a q")), writes=["a2"], dma_key="a2")
                S.add("sp", lambda e, c8=c8: e.dma_start(out=b2, in_=s5_b2[j, c8].rearrange("a p q -> p a q")), writes=["b2"], dma_key="b2")
                S.add("sp", lambda e, c8=c8: e.dma_start(out=c2, in_=s5_c2[j, c8].rearrange("a p q -> p a q")), writes=["c2"], dma_key="c2")
                lam_re, dt_, x1, mag, th, ang, sn, cs, den, ere, zre, zim = wk_
                P = "s5p"
                def V(fn, eng="dve", r=(P,), w=(P,)):
                    S.add(eng, fn, reads=list(r), writes=list(w))
                V(lambda e: e.tensor_scalar(out=lam_re, in0=a2[:, 0, :], scalar1=-1e-4, scalar2=None, op0=ALU.min), r=("a2", P))
                V(lambda e: e.activation(out=dt_, in_=a2[:, 2, :], func=AF.Exp), eng="act", r=("a2", P))
                V(lambda e: e.tensor_tensor(out=x1, in0=lam_re, in1=dt_, op=ALU.mult))
                V(lambda e: e.activation(out=mag, in_=x1, func=AF.Exp), eng="act")
                V(lambda e: e.tensor_tensor(out=th, in0=a2[:, 1, :], in1=dt_, op=ALU.mult), r=("a2", P))
                V(lambda e: e.tensor_copy(out=ang, in_=th))
                sin_of(sn, ang, 64, [P], [P])
                V(lambda e: e.tensor_scalar(out=ang, in0=th, scalar1=0.5 * PI, scalar2=None, op0=ALU.add))
                sin_of(cs, ang, 64, [P], [P])
                V(lambda e: e.tensor_tensor(out=sn, in0=sn, in1=mag, op=ALU.mult))
                V(lambda e: e.tensor_tensor(out=cs, in0=cs, in1=mag, op=ALU.mult))
                V(lambda e: e.tensor_tensor(out=den, in0=lam_re, in1=lam_re, op=ALU.mult))
                V(lambda e: e.tensor_tensor(out=x1, in0=a2[:, 1, :], in1=a2[:, 1, :], op=ALU.mult), r=("a2", P))
                V(lambda e: e.tensor_tensor(out=den, in0=den, in1=x1, op=ALU.add))
                V(lambda e: e.reciprocal(out=den, in_=den))
                V(lambda e: e.tensor_scalar(out=ere, in0=cs, scalar1=-1.0, scalar2=None, op0=ALU.add))
                V(lambda e: e.tensor_tensor(out=zre, in0=ere, in1=lam_re, op=ALU.mult))
                V(lambda e: e.tensor_tensor(out=x1, in0=sn, in1=a2[:, 1, :], op=ALU.mult), r=("a2", P))
                V(lambda e: e.tensor_tensor(out=zre, in0=zre, in1=x1, op=ALU.add))
                V(lambda e: e.tensor_tensor(out=zre, in0=zre, in1=den, op=ALU.mult))
                V(lambda e: e.tensor_tensor(out=zim, in0=sn, in1=lam_re, op=ALU.mult))
                V(lambda e: e.tensor_tensor(out=x1, in0=ere, in1=a2[:, 1, :], op=ALU.mult), r=("a2", P))
                V(lambda e: e.tensor_tensor(out=zim, in0=zim, in1=x1, op=ALU.subtract))
                V(lambda e: e.tensor_tensor(out=zim, in0=zim, in1=den, op=ALU.mult))
                V(lambda e: e.tensor_tensor(out=Bf[:, 0:64], in0=zre, in1=b2[:, 0, :], op=ALU.mult), r=("b2", P), w=("Bf",))
                V(lambda e: e.tensor_tensor(out=x1, in0=zim, in1=b2[:, 1, :], op=ALU.mult), r=("b2", P))
                V(lambda e: e.tensor_tensor(out=Bf[:, 0:64], in0=Bf[:, 0:64], in1=x1, op=ALU.subtract), r=("Bf", P), w=("Bf",))
                V(lambda e: e.tensor_tensor(out=Bf[:, 64:128], in0=zre, in1=b2[:, 1, :], op=ALU.mult), r=("b2", P, "Bf"), w=("Bf",))
                V(lambda e: e.tensor_tensor(out=x1, in0=zim, in1=b2[:, 0, :], op=ALU.mult), r=("b2", P, "Bf"))
                V(lambda e: e.tensor_tensor(out=Bf[:, 64:128], in0=Bf[:, 64:128], in1=x1, op=ALU.add), r=("Bf", P), w=("Bf",))
                V(lambda e: e.tensor_copy(out=Bf2[:, 0:64], in_=Bf[:, 64:128]), r=("Bf",), w=("Bf2",))
                V(lambda e: e.tensor_scalar(out=Bf2[:, 64:128], in0=Bf[:, 0:64], scalar1=-1.0, scalar2=None, op0=ALU.mult), r=("Bf", "Bf2"), w=("Bf2",))
                V(lambda e: e.tensor_scalar(out=CA, in0=c2[:, 0, :], scalar1=sgn, scalar2=None, op0=ALU.mult), r=("c2", "cst"), w=("CA",))
                V(lambda e: e.tensor_scalar(out=CB, in0=c2[:, 1, :], scalar1=-1.0, scalar2=None, op0=ALU.mult), r=("c2",), w=("CB",))
                for g_ in range(8):
                    g = c8 * 8 + g_
                    V(lambda e, g_=g_: e.tensor_scalar(out=ZB[:, g_, :], in0=Bf, scalar1=mask8[:, g_:g_ + 1], scalar2=None, op0=ALU.mult),
                      r=("Bf", "cst"), w=("Z",))
                    V(lambda e, g_=g_: e.tensor_scalar(out=ZB2[:, g_, :], in0=Bf2, scalar1=mask8[:, g_:g_ + 1], scalar2=None, op0=ALU.mult),
                      r=("Bf2", "cst", "Z"), w=("Z",))
                S.add("pool", lambda e: e.memset(ZCA, 0.0), reads=["Z"], writes=["ZC"])
                S.add("pool", lambda e: e.memset(ZCB, 0.0), reads=["ZC"], writes=["ZC"])
                for g_ in range(8):
                    g = c8 * 8 + g_
                    cs_ = slice(g_ * 16, g_ * 16 + 16)
                    V(lambda e, g_=g_, cs_=cs_: e.tensor_copy(out=ZCA[:, g_, cs_], in_=CA[:, cs_]), r=("CA", "ZC"), w=("ZC",))
                    V(lambda e, g_=g_, cs_=cs_: e.tensor_copy(out=ZCB[:, g_, cs_], in_=CB[:, cs_]), r=("CB", "ZC"), w=("ZC",))
                    V(lambda e, g_=g_, g=g: e.tensor_scalar(out=ROT[:, g_, :], in0=ident, scalar1=cosL[:, g:g + 1], scalar2=None, op0=ALU.mult),
                      r=("cst", "cosL", "ROT"), w=("ROT",))
                    V(lambda e, g_=g_, g=g: e.scalar_tensor_tensor(out=ROT[:, g_, :], in0=swapm, scalar=ssinL[:, g:g + 1], in1=ROT[:, g_, :],
                                                                  op0=ALU.mult, op1=ALU.add),
                      r=("cst", "ssinL", "ROT"), w=("ROT",))
                    V(lambda e, g=g: e.tensor_scalar(out=targ, in0=iota1, scalar1=th1[:, g:g + 1], scalar2=None, op0=ALU.mult),
                      r=("cst", "th1", "targ", "TAB"), w=("targ",))
                    sin_of(SIN[:, g_, :], targ, TB, ["targ"], ["TAB"])
                    V(lambda e, g=g: e.tensor_scalar(out=targ, in0=iota1, scalar1=th1[:, g:g + 1], scalar2=0.5 * PI, op0=ALU.mult, op1=ALU.add),
                      r=("cst", "th1", "targ", "TAB"), w=("targ",))
                    sin_of(COS[:, g_, :], targ, TB, ["targ"], ["TAB"])
                    V(lambda e, g_=g_, g=g: e.tensor_scalar(out=RT[:, g_, :], in0=iota1, scalar1=0.0, scalar2=r1[:, g:g + 1],
                                                            op0=ALU.mult, op1=ALU.add),
                      eng="pool", r=("cst", "r1", "TAB"), w=("TAB",))
                yi = c8 % 2
                for tb in range(NB):
                    tsl = slice(tb * TB, (tb + 1) * TB)
                    psy, psyk = PSY[tb % 2]
                    for g_ in range(8):
                        b = it % 2
                        it += 1
                        p1, p1k = PS1[b]
                        p2, p2k = PS2[b]
                        kb = bk[b]
                        S.add("pe", lambda e, p1=p1, g_=g_, tsl=tsl, ui=ui: e.matmul(p1[:], lhsT=ZB[:, g_, :], rhs=u[ui][:, tsl], start=True, stop=True),
                              reads=["Z", uk[ui]], writes=[p1k])
                        S.add("pe", lambda e, p2=p2, g_=g_, tsl=tsl, ui=ui: e.matmul(p2[:], lhsT=ZB2[:, g_, :], rhs=u[ui][:, tsl], start=True, stop=True),
                              reads=["Z", uk[ui]], writes=[p2k])
                        S.add("dve", lambda e, b=b, p1=p1, g_=g_: e.tensor_tensor(out=t1[b], in0=p1[:], in1=COS[:, g_, :], op=ALU.mult),
                              reads=[p1k, "TAB"], writes=[kb[0]])
                        S.add("dve", lambda e, b=b, p2=p2, g_=g_: e.tensor_tensor(out=t2[b], in0=p2[:], in1=SIN[:, g_, :], op=ALU.mult),
                              reads=[p2k, "TAB"], writes=[kb[1]])
                        S.add("pool", lambda e, b=b: e.tensor_tensor(out=vt[b], in0=t1[b], in1=t2[b], op=ALU.add),
                              reads=[kb[0], kb[1]], writes=[kb[2]])
                        init = 0.0 if tb == 0 else st8[:, g_:g_ + 1]
                        S.add("dve", lambda e, b=b, g_=g_, init=init: e.tensor_tensor_scan(out=vv[b], data0=RT[:, g_, :], data1=vt[b], initial=init,
                                                                                         op0=ALU.mult, op1=ALU.add),
                              reads=[kb[2], "TAB", ("st8", g_)], writes=[kb[3]])
                        if tb < NB - 1:
                            S.add("pe", lambda e, b=b, g_=g_: e.matmul(pst32[:, g_:g_ + 1], lhsT=ROT[:, g_, :], rhs=vv[b][:, TB - 1:TB], start=True, stop=True),
                                  reads=[kb[3], "ROT"], writes=[("pst", g_)])
                            S.add("act", lambda e, g_=g_: e.activation(out=st8[:, g_:g_ + 1], in_=pst32[:, g_:g_ + 1], func=AF.Copy),
                                  reads=[("pst", g_)], writes=[("st8", g_)])
                        S.add("pool", lambda e, b=b, g_=g_: e.tensor_tensor(out=w1[b], in0=vv[b], in1=COS[:, g_, :], op=ALU.mult),
                              reads=[kb[3], "TAB"], writes=[kb[4]])
                        S.add("pool", lambda e, b=b, g_=g_: e.tensor_tensor(out=w2[b], in0=vv[b], in1=SIN[:, g_, :], op=ALU.mult),
                              reads=[kb[3], "TAB"], writes=[kb[5]])
                        S.add("pe", lambda e, b=b, g_=g_, psy=psy: e.matmul(psy[:], lhsT=ZCA[:, g_, :], rhs=w1[b], start=(g_ == 0), stop=False),
                              reads=[kb[4], "ZC"], writes=[psyk])
                        S.add("pe", lambda e, b=b, g_=g_, psy=psy: e.matmul(psy[:], lhsT=ZCB[:, g_, :], rhs=w2[b], start=False, stop=(g_ == 7)),
                              reads=[kb[5], "ZC"], writes=[psyk])
                    S.add("dve", lambda e, psy=psy, tsl=tsl, c8=c8, ui=ui: e.scalar_tensor_tensor(out=yy, in0=u[ui][:, tsl], scalar=ddc[:, c8:c8 + 1], in1=psy[:],
                                                                                         op0=ALU.mult, op1=ALU.add),
                          reads=[psyk, uk[ui], "ddc", "yy"], writes=["yy"])
                    S.add("act", lambda e: e.activation(out=gsq, in_=yy, func=AF.Square), reads=["yy", "gsq"], writes=["gsq"])
                    S.add("dve", lambda e: e.tensor_scalar(out=gsq, in0=gsq, scalar1=0.044715, scalar2=1.0, op0=ALU.mult, op1=ALU.add),
                          reads=["gsq"], writes=["gsq"])
                    S.add("dve", lambda e: e.tensor_tensor(out=gin, in0=gsq, in1=yy, op=ALU.mult), reads=["gsq", "yy", "gin"], writes=["gin"])
                    S.add("act", lambda e: e.activation(out=gin, in_=gin, func=AF.Sigmoid, scale=1.5957691216057308), reads=["gin"], writes=["gin"])
                    S.add("pool", lambda e, tsl=tsl, yi=yi: e.tensor_tensor(out=yb[yi][:, tsl], in0=yy, in1=gin, op=ALU.mult),
                          reads=["gin", "yy"], writes=[ybk[yi]])
                S.add("sp", lambda e, c8=c8, yi=yi: e.dma_start(out=yT[c8], in_=yb[yi]), reads=[ybk[yi]], writes=[("yT", c8)], dma_key=ybk[yi] + "st")
            phase_end()

        def hgrn_phase(jh, l):
            phase_begin()
            CM = A16(T)
            S.add("pool", lambda e: e.dma_start(out=CM, in_=cmask_d, max_dma_last_dim=8192), writes=["CM"], dma_key="CM")
            qr = A32(T)
            fr = A32(T)
            vr = A32(T)
            lf = A32(T)
            cum = A32(T)
            kk = A32(T)
            qt = A16(T)
            kt = A16(T)
            vb = A16(T)
            osb = A32(T)
            ecl = A32(NCK)
            ecm = A32(NCK + 1)
            ecd = A32(NCK)
            PTs = [A16(CH) for _ in range(2)]
            vtok = [A16(128) for _ in range(2)]
            ktok = [A16(128) for _ in range(2)]
            Sst = A32(128)
            Sbf = A16(128)
            stmp = A32(128)
            osq = [A16(512) for _ in range(2)]
            sdh = A32(512)
            rsh = A32(512)
            gn = A32(KC)
            S.add("sp", lambda e: e.dma_start(out=gn, in_=hg_gn[jh]), writes=["gn"], dma_key="gn")
            cum3 = cum.rearrange("p (c t) -> p c t", t=CH)
            lf3 = lf.rearrange("p (c t) -> p c t", t=CH)
            for hd in range(KC):
                lo = l * KC + hd
                H = "hg"
                S.add("sp", lambda e, hd=hd: e.dma_start(out=qr, in_=qfv[hd]), reads=[("qfv", hd)], writes=["qr"], dma_key="qr")
                S.add("sp", lambda e, hd=hd: e.dma_start(out=fr, in_=qfv[KC + hd]), reads=[("qfv", KC + hd)], writes=["fr"], dma_key="fr")
                S.add("sp", lambda e, hd=hd: e.dma_start(out=vr, in_=qfv[2 * KC + hd]), reads=[("qfv", 2 * KC + hd)], writes=["vr"], dma_key="vr")
                S.add("act", lambda e: e.activation(out=qr, in_=qr, func=AF.Silu), reads=["qr"], writes=["qr"])
                S.add("act", lambda e: e.activation(out=fr, in_=fr, func=AF.Sigmoid), reads=["fr"], writes=["fr"])
                S.add("dve", lambda e, lo=lo: e.tensor_scalar(out=fr, in0=fr, scalar1=omv[:, lo:lo + 1], scalar2=lbv[:, lo:lo + 1], op0=ALU.mult, op1=ALU.add),
                      reads=["fr", "omv", "lbv"], writes=["fr"])
                S.add("act", lambda e: e.activation(out=lf, in_=fr, func=AF.Ln), reads=["fr", "lf"], writes=["lf"])
                S.add("dve", lambda e: e.tensor_scalar(out=kk, in0=fr, scalar1=-1.0, scalar2=1.0, op0=ALU.mult, op1=ALU.add),
                      reads=["fr", "kk"], writes=["kk"])
                S.add("dve", lambda e: e.tensor_tensor_scan(out=cum, data0=CM, data1=lf, initial=0.0, op0=ALU.mult, op1=ALU.add),
                      reads=["CM", "lf", "cum"], writes=["cum"])
                cmid = cum3[:, :, CH // 2 - 1]
                clast = cum3[:, :, CH - 1]
                S.add("act", lambda e: e.activation(out=ecl, in_=clast, func=AF.Exp), reads=["cum", "ecl"], writes=["ecl"])
                S.add("act", lambda e: e.activation(out=ecm[:, 0:NCK], in_=cmid, func=AF.Exp), reads=["cum", "ecm"], writes=["ecm"])
                S.add("dve", lambda e: e.tensor_tensor(out=ecd, in0=clast, in1=cmid, op=ALU.subtract), reads=["cum", "ecd"], writes=["ecd"])
                S.add("act", lambda e: e.activation(out=ecd, in_=ecd, func=AF.Exp), reads=["ecd"], writes=["ecd"])
                S.add("dve", lambda e: e.tensor_tensor(out=lf3, in0=cum3, in1=cum3[:, :, CH // 2 - 1:CH // 2].to_broadcast([128, NCK, CH]),
                                                       op=ALU.subtract),
                      reads=["cum", "lf"], writes=["lf"])
                S.add("act", lambda e: e.activation(out=cum, in_=lf, func=AF.Exp), reads=["lf", "cum", "ecl", "ecm", "ecd"], writes=["cum"])
                S.add("act", lambda e: e.activation(out=lf, in_=lf, func=AF.Exp, scale=-1.0), reads=["lf", "cum"], writes=["lf"])
                S.add("dve", lambda e: e.tensor_tensor(out=qt, in0=qr, in1=cum, op=ALU.mult), reads=["qr", "cum", "qt"], writes=["qt"])
                S.add("pool", lambda e: e.tensor_tensor(out=kt, in0=kk, in1=lf, op=ALU.mult), reads=["kk", "lf", "kt"], writes=["kt"])
                S.add("act", lambda e: e.activation(out=vb, in_=vr, func=AF.Copy), reads=["vr", "vb"], writes=["vb"])
                for ch in range(NCK):
                    csl = slice(ch * CH, (ch + 1) * CH)
                    b = ch % 2
                    psS, psSk = PSF[b], "psf%d" % b
                    psO, psOk = PSF[2 + b], "psf%d" % (2 + b)
                    psT, psTk = PSF[4], "psf4"
                    pbv = PSB[0][0:CH, b * 128:(b + 1) * 128]
                    pbk = PSB[1][0:CH, b * 128:(b + 1) * 128]
                    S.add("pe", lambda e, psS=psS, csl=csl: e.matmul(psS[0:CH, 0:CH], lhsT=kt[:, csl], rhs=qt[:, csl], start=True, stop=True),
                          reads=["kt", "qt"], writes=[psSk])
                    S.add("dve", lambda e, psS=psS, b=b: e.tensor_tensor(out=PTs[b][0:CH, :], in0=psS[0:CH, 0:CH], in1=maskT[0:CH, 0:CH], op=ALU.mult),
                          reads=[psSk, "cst", ("PT", b)], writes=[("PT", b)])
                    S.add("pe", lambda e, pbv=pbv, csl=csl: e.transpose(pbv, vb[:, csl], identb), reads=["vb", "identb"], writes=[("pbv", b)])
                    S.add("act", lambda e, pbv=pbv, b=b: e.activation(out=vtok[b][0:CH, :], in_=pbv, func=AF.Copy), reads=[("pbv", b), ("vtok", b)], writes=[("vtok", b)])
                    S.add("pe", lambda e, pbk=pbk, csl=csl: e.transpose(pbk, kt[:, csl], identb), reads=["kt", "identb"], writes=[("pbk", b)])
                    S.add("act", lambda e, pbk=pbk, b=b: e.activation(out=ktok[b][0:CH, :], in_=pbk, func=AF.Copy), reads=[("pbk", b), ("ktok", b)], writes=[("ktok", b)])
                    S.add("pe", lambda e, psO=psO, b=b, ch=ch: e.matmul(psO[:, 0:CH], lhsT=vtok[b][0:CH, :], rhs=PTs[b][0:CH, :], start=True, stop=(ch == 0)),
                          reads=[("vtok", b), ("PT", b)], writes=[psOk])
                    if ch > 0:
                        S.add("pe", lambda e, psO=psO, csl=csl: e.matmul(psO[:, 0:CH], lhsT=Sbf, rhs=qt[:, csl], start=False, stop=True),
                              reads=["Sbf", "qt"], writes=[psOk])
                    S.add("act", lambda e, psO=psO, csl=csl: e.activation(out=osb[:, csl], in_=psO[:, 0:CH], func=AF.Copy),
                          reads=[psOk, "osb"], writes=["osb"])
                    if ch < NCK - 1:
                        S.add("pe", lambda e, psT=psT, b=b: e.matmul(psT[:, 0:128], lhsT=ktok[b][0:CH, :], rhs=vtok[b][0:CH, :], start=True, stop=True),
                              reads=[("ktok", b), ("vtok", b)], writes=[psTk])
                        if ch == 0:
                            S.add("dve", lambda e, psT=psT, ch=ch: e.tensor_scalar(out=Sst, in0=psT[:, 0:128], scalar1=ecd[:, ch:ch + 1], scalar2=None, op0=ALU.mult),
                                  reads=[psTk, "ecd", "Sst"], writes=["Sst"])
                        else:
                            S.add("dve", lambda e, psT=psT, ch=ch: e.tensor_scalar(out=stmp, in0=psT[:, 0:128], scalar1=ecd[:, ch:ch + 1], scalar2=None, op0=ALU.mult),
                                  reads=[psTk, "ecd", "stmp"], writes=["stmp"])
                            S.add("dve", lambda e, ch=ch: e.scalar_tensor_tensor(out=Sst, in0=Sst, scalar=ecl[:, ch:ch + 1], in1=stmp, op0=ALU.mult, op1=ALU.add),
                                  reads=["stmp", "ecl", "Sst"], writes=["Sst"])
                        S.add("dve", lambda e, ch=ch: e.tensor_scalar(out=Sbf, in0=Sst, scalar1=ecm[:, ch + 1:ch + 2], scalar2=None, op0=ALU.mult),
                              reads=["Sst", "ecm", "Sbf"], writes=["Sbf"])
                for tb in range(T // 512):
                    tsl = slice(tb * 512, (tb + 1) * 512)
                    b = tb % 2
                    S.add("act", lambda e, b=b, tsl=tsl: e.activation(out=osq[b], in_=osb[:, tsl], func=AF.Square), reads=["osb", ("osq", b)], writes=[("osq", b)])
                    S.add("pe", lambda e, b=b: e.matmul(PSF[5][:], lhsT=onesb, rhs=osq[b], start=True, stop=True), reads=[("osq", b), "onesb"], writes=["psf5"])
                    S.add("act", lambda e: e.activation(out=sdh, in_=PSF[5][:], func=AF.Sqrt, scale=1.0 / 128, bias=EPS), reads=["psf5", "sdh"], writes=["sdh"])
                    S.add("dve", lambda e: e.reciprocal(out=rsh, in_=sdh), reads=["sdh", "rsh"], writes=["rsh"])
                    S.add("dve", lambda e, tsl=tsl, hd=hd: e.scalar_tensor_tensor(out=osb[:, tsl], in0=osb[:, tsl], scalar=gn[:, hd:hd + 1], in1=rsh,
                                                                                 op0=ALU.mult, op1=ALU.mult),
                          reads=["rsh", "gn", "osb"], writes=["osb"])
                S.add("sp", lambda e, hd=hd: e.dma_start(out=onT[hd], in_=osb), reads=["osb"], writes=[("onT", hd)], dma_key="osbst")
            phase_end()

        for l in range(DEPTH if KSTOP < 0 else KSTOP):
            j = l // 2
            is_s5 = (l % 2 == 0)
            phase_begin()
            Dn = Dense(need_g=False)
            for tt in range(NT):
                rmsnorm_to_aT(Dn, hT, "hT", tt, l, 0)
                if is_s5:
                    for kc in range(KC):
                        S.add("sp", lambda e, kc=kc, tt=tt, Dn=Dn: e.dma_start(out=uT[kc][:, tslice(tt)], in_=Dn.aT[kc]),
                              reads=[(Dn.aTk, kc)], writes=[("uT", kc)], dma_key=Dn.aTk + "st%d" % (kc % 4))
                else:
                    def win_cons(n, outs, tt=tt):
                        (ps, pk), = outs
                        ms, mk = Dn.mslot()
                        S.add("act", lambda e: e.activation(out=ms, in_=ps[:], func=AF.Copy), reads=[pk], writes=[mk])
                        S.add("sp", lambda e: e.dma_start(out=qfv[n][:, tslice(tt)], in_=ms), reads=[mk], writes=[("qfv", n)], dma_key=mk + "st")
                    linear(Dn, hg_win[j], 4 * KC, KC, False, win_cons, cache=wc_win[j], cname="wc_win", first=(tt == 0))
            phase_end()
            if is_s5:
                s5_phase(j)
            else:
                hgrn_phase(j, l)
            phase_begin()
            Dn = Dense(need_g=True)
            for tt in range(NT):
                if is_s5:
                    for kc in range(KC):
                        S.add("sp", lambda e, kc=kc, tt=tt, Dn=Dn: e.dma_start(out=Dn.aT[kc], in_=yT[kc][:, tslice(tt)]),
                              reads=[("yT", kc)], writes=[(Dn.aTk, kc)], dma_key=Dn.aTk + "ld%d" % (kc % 4))

                    def glu_cons(n, outs):
                        (pv, pvk), (pg, pgk) = outs
                        t, tk = Dn.tslot()
                        ms, mk = Dn.mslot()
                        S.add("act", lambda e: e.activation(out=t, in_=pg[:], func=AF.Sigmoid), reads=[pgk], writes=[tk])
                        S.add("dve", lambda e: e.tensor_tensor(out=ms, in0=pv[:], in1=t, op=ALU.mult), reads=[pvk, tk], writes=[mk])
                        out_chunk(Dn, n, KC - 1, ms, mk, already_sbuf=True)
                    linear(Dn, s5_wg[j], KC, KC, True, glu_cons, cache=wc_glu[j], cname="wc_glu", first=(tt == 0))
                else:
                    for kc in range(KC):
                        os_, ok = load_chunk(Dn, onT[kc][:, tslice(tt)], ("onT", kc))
                        gs, gk = load_chunk(Dn, qfv[3 * KC + kc][:, tslice(tt)], ("qfv", 3 * KC + kc))
                        S.add("act", lambda e, gs=gs: e.activation(out=gs, in_=gs, func=AF.Silu), reads=[gk], writes=[gk])
                        S.add("dve", lambda e, os_=os_, gs=gs, kc=kc, Dn=Dn: e.tensor_tensor(out=Dn.aT[kc], in0=os_, in1=gs, op=ALU.mult),
                              reads=[ok, gk], writes=[(Dn.aTk, kc)])

                    def wo_cons(n, outs):
                        (ps, pk), = outs
                        out_chunk(Dn, n, KC - 1, ps[:], pk)
                    linear(Dn, hg_wout[j], KC, KC, False, wo_cons, cache=wc_wout[j], cname="wc_wout", first=(tt == 0))
                resid_update(Dn, tt, l, 1, hT, "hT")
                ffn_tile(Dn, tt, l, l == DEPTH - 1)
            fkeys = [k for k in S.dma_count if k.endswith("st") and k.startswith("hs")] if l == DEPTH - 1 else ()
            phase_end(final=fkeys)
        build_program.stats = (len(S.all_ops), len(S.dma_count), {e: len(S.ops[e]) for e in ENGS})
    return nc


def tile_w(W):
    K, N = W.shape
    return np.ascontiguousarray(W.reshape(K // 128, 128, N // 128, 128).transpose(2, 1, 0, 3))


def host_consts(T):
    c = np.zeros((128, 5 * 128 + 8 + 1 + 512), np.float32)
    c[:, 0:128] = np.eye(128)
    sw = np.zeros((128, 128), np.float32)
    for p in range(64):
        sw[p, 64 + p] = 1
        sw[64 + p, p] = 1
    c[:, 128:256] = sw
    s = np.arange(128)[:, None]
    t = np.arange(128)[None, :]
    c[:, 256:384] = (s <= t)
    c[:, 384:512] = 1.0
    for g in range(8):
        c[g * 16:(g + 1) * 16, 640 + g] = 1.0
    c[0:64, 648] = 1.0
    c[64:128, 648] = -1.0
    c[:, 649:649 + 512] = np.arange(1, 513, dtype=np.float32)[None, :]
    cm = np.ones((128, T), np.float32)
    cm[:, 0::HG_CH] = 0.0
    return c, cm


def prep_shared(inp, cfg):
    D, KC, G, DEPTH, NS5, NHG = cfg.D, cfg.KC, cfg.G, cfg.DEPTH, cfg.NS5, cfg.NHG
    f = lambda a: np.ascontiguousarray(np.asarray(a, dtype=np.float32))
    m = {}
    ng = f(inp["norm_gains"])
    m["gains"] = f(ng.reshape(DEPTH, 4, KC, 128).transpose(3, 0, 1, 2).reshape(128, DEPTH * 4 * KC))
    c, cm = host_consts(cfg.T)
    m["cst"], m["cmask"] = c, cm
    lbl = f(inp["hgrn_lb_logits"])
    m["lbl"] = f(lbl.reshape(DEPTH, KC, 128).transpose(2, 0, 1).reshape(128, DEPTH * KC))
    are, aim, ldt = f(inp["s5_a_re"]), f(inp["s5_a_im"]), f(inp["s5_log_dt"])
    a1 = np.zeros((NS5, 3, 128, G), np.float32)
    a1[:, 0] = np.concatenate([are.transpose(0, 2, 1)] * 2, axis=1)
    a1[:, 1] = np.concatenate([aim.transpose(0, 2, 1)] * 2, axis=1)
    a1[:, 2] = np.broadcast_to(ldt[:, None, :], (NS5, 128, G))
    m["s5_a1"] = a1
    a2 = np.zeros((NS5, KC, 3, 128, 64), np.float32)
    rep = lambda z: np.broadcast_to(z.reshape(NS5, KC, 8, 1, 64), (NS5, KC, 8, 16, 64)).reshape(NS5, KC, 128, 64)
    a2[:, :, 0] = rep(are)
    a2[:, :, 1] = rep(aim)
    a2[:, :, 2] = rep(np.broadcast_to(ldt[:, :, None], (NS5, G, 64)))
    m["s5_a2"] = a2
    bre, bim = f(inp["s5_b_re"]), f(inp["s5_b_im"])
    bt = lambda z: z.reshape(NS5, KC, 8, 64, 16).transpose(0, 1, 2, 4, 3).reshape(NS5, KC, 128, 64)
    m["s5_b2"] = f(np.stack([bt(bre), bt(bim)], axis=2))
    cre, cim = f(inp["s5_c_re"]), f(inp["s5_c_im"])
    ct = lambda z: z.reshape(NS5, KC, 8, 16, 64).transpose(0, 1, 4, 2, 3).reshape(NS5, KC, 64, 128)
    c0 = np.concatenate([ct(cre), ct(cim)], axis=2)
    c1 = np.concatenate([ct(cim), ct(cre)], axis=2)
    m["s5_c2"] = f(np.stack([c0, c1], axis=2))
    m["s5_dd"] = f(f(inp["s5_d"]).reshape(NS5, KC, 128).transpose(0, 2, 1))
    wg = f(inp["s5_w_glu"])
    m["s5_wg"] = f(np.stack([np.stack([tile_w(wg[j][:, :D]), tile_w(wg[j][:, D:])], axis=3) for j in range(NS5)]))
    m["hg_win"] = f(np.stack([tile_w(f(inp["hgrn_w_in"][j])) for j in range(NHG)]))
    m["hg_wout"] = f(np.stack([tile_w(f(inp["hgrn_w_out"][j])) for j in range(NHG)]))
    m["hg_gn"] = f(f(inp["hgrn_g_norm"]).reshape(NHG, KC, 128).transpose(0, 2, 1))
    gu = inp["ffn_w_gate_up"]
    DFF = cfg.DFF
    m["ff_gu"] = f(np.stack([np.stack([tile_w(f(gu[l][:, :DFF])), tile_w(f(gu[l][:, DFF:]))], axis=3) for l in range(DEPTH)]))
    m["ff_dn"] = f(np.stack([tile_w(f(inp["ffn_w_down"][l])) for l in range(DEPTH)]))
    return m


_CACHE = {}


def run(inp, cfg):
    x = np.asarray(inp["x"], dtype=np.float32)
    B = x.shape[0]
    shared = prep_shared(inp, cfg)
    in_maps = []
    for b in range(B):
        m = dict(shared)
        m["xT"] = np.ascontiguousarray(x[b].T.reshape(cfg.KC, 128, cfg.T))
        in_maps.append(m)
    kk = (cfg.D, cfg.T, cfg.DFF, cfg.DEPTH)
    if kk not in _CACHE:
        _CACHE[kk] = build_program(cfg)
    nc = _CACHE[kk]
    res = run_bass_kernel_spmd(nc, in_maps, core_ids=list(range(B)))
    out = np.stack([res.results[b]["outT"].reshape(cfg.D, cfg.T).T for b in range(B)], axis=0)
    if DEBUG:
        run.dbg = res.results
    return np.ascontiguousarray(out.astype(np.float32))


def kernel(**inputs):
    cfg = Cfg(D=4096, T=4096, DFF=11008, DEPTH=4)
    return run(inputs, cfg)
```

```python
import contextlib
import math
import numpy as np
import concourse.bass as bass
import concourse.mybir as mybir
from concourse.bass_utils import run_bass_kernel_spmd

F32 = mybir.dt.float32
BF16 = mybir.dt.bfloat16
AF = mybir.ActivationFunctionType
ALU = mybir.AluOpType
ENGS = ("pe", "dve", "act", "pool", "sp")
PI = math.pi
import os
NOWAIT = bool(int(os.environ.get('NOWAIT', '0')))
ONLY = os.environ.get('ONLY', '')
DEBUG = bool(int(os.environ.get('KDEBUG', '0')))
KSTOP = int(os.environ.get('KSTOP', '-1'))
EPS = 1e-6
HG_CH = 64


class Op:
    __slots__ = ("eng", "fn", "dma_key", "deps", "milestone", "sig_val", "is_dma")

    def __init__(self, eng, fn, dma_key):
        self.eng = eng
        self.fn = fn
        self.dma_key = dma_key
        self.is_dma = dma_key is not None
        self.deps = []
        self.milestone = False
        self.sig_val = None


class Sched:
    def __init__(self, nc):
        self.nc = nc
        self.ops = {e: [] for e in ENGS}
        self.last_writer = {}
        self.readers = {}
        self.dma_count = {}
        self.last_dma = {}
        self.all_ops = []
        self.pending = {e: [] for e in ENGS}

    def barrier(self):
        deps = []
        for e in ENGS:
            if self.ops[e]:
                deps.append(self.ops[e][-1])
        deps.extend(self.last_dma.values())
        for e in ENGS:
            self.pending[e] = list(deps)
        self.last_writer = {}
        self.readers = {}

    def add(self, eng, fn, reads=(), writes=(), dma_key=None):
        op = Op(eng, fn, dma_key)
        deps = list(self.pending[eng])
        self.pending[eng] = []
        for r in reads:
            lw = self.last_writer.get(r)
            if lw is not None:
                deps.append(lw)
        for w in writes:
            lw = self.last_writer.get(w)
            if lw is not None:
                deps.append(lw)
            deps.extend(self.readers.get(w, ()))
        if op.is_dma and dma_key in self.last_dma:
            deps.append(self.last_dma[dma_key])
        for r in reads:
            self.readers.setdefault(r, []).append(op)
        for w in writes:
            self.last_writer[w] = op
            self.readers[w] = []
        seen = set()
        for d in deps:
            if d is op or id(d) in seen:
                continue
            seen.add(id(d))
            if d.eng == "pe" and eng == "pe" and not d.is_dma and not op.is_dma:
                continue
            op.deps.append(d)
        if op.is_dma:
            c = self.dma_count.get(dma_key, 0) + 1
            self.dma_count[dma_key] = c
            op.sig_val = 16 * c
            self.last_dma[dma_key] = op
        self.ops[eng].append(op)
        self.all_ops.append(op)
        return op

    def setup(self, stack):
        self.stack = stack
        self.esem = {e: stack.enter_context(self.nc.semaphore("s_" + e)) for e in ENGS}
        self.dsem = {}
        self.counters = {e: 0 for e in ENGS}
        self.seen_e = {e: {x: 0 for x in ENGS} for e in ENGS}
        self.seen_d = {e: {} for e in ENGS}
        self.cur = {e: [] for e in ENGS}
        self.emitted = 0

    def flush(self, final_dma_keys=()):
        nc = self.nc
        new_ops = {e: self.ops[e][len(self.cur[e]):] for e in ENGS}
        for e in ENGS:
            for op in new_ops[e]:
                for d in op.deps:
                    if not d.is_dma:
                        d.milestone = True
            if new_ops[e]:
                last = new_ops[e][-1]
                if not last.is_dma:
                    last.milestone = True
        for e in ENGS:
            for op in new_ops[e]:
                if op.milestone and not op.is_dma:
                    self.counters[e] += 1
                    op.sig_val = self.counters[e]
                if op.is_dma and op.dma_key not in self.dsem:
                    self.dsem[op.dma_key] = self.stack.enter_context(nc.semaphore("d_%d" % len(self.dsem)))
        esem, dsem = self.esem, self.dsem
        with nc.Block() as block:
            def run(e, engobj):
                seen_e = self.seen_e[e]
                seen_d = self.seen_d[e]
                for op in new_ops[e]:
                    for d in op.deps:
                        if d.is_dma:
                            if seen_d.get(d.dma_key, 0) < d.sig_val:
                                engobj.wait_ge(dsem[d.dma_key], d.sig_val)
                                seen_d[d.dma_key] = d.sig_val
                        else:
                            if seen_e[d.eng] < d.sig_val:
                                engobj.wait_ge(esem[d.eng], d.sig_val)
                                seen_e[d.eng] = d.sig_val
                    ins = op.fn(engobj)
                    op.fn = None
                    if op.is_dma:
                        ins.then_inc(dsem[op.dma_key], 16)
                    elif op.milestone:
                        ins.then_inc(esem[e], 1)
                if e == "sp":
                    for k in final_dma_keys:
                        engobj.wait_ge(dsem[k], 16 * self.dma_count[k])

            @block.tensor
            def _(eng):
                run("pe", eng)

            @block.vector
            def _(eng):
                run("dve", eng)

            @block.scalar
            def _(eng):
                run("act", eng)

            @block.gpsimd
            def _(eng):
                run("pool", eng)

            @block.sync
            def _(eng):
                run("sp", eng)
        for e in ENGS:
            self.cur[e] = list(self.ops[e])
        self.barrier()


class Cfg:
    def __init__(self, D=4096, T=4096, DFF=11008, DEPTH=4):
        self.D, self.T, self.DFF, self.DEPTH = D, T, DFF, DEPTH
        self.KC = D // 128
        self.TT = 512
        self.NT = T // 512
        self.JF = DFF // 128
        self.G = D // 16
        self.NS5 = (DEPTH + 1) // 2
        self.NHG = DEPTH // 2
        self.ARENA = 48 * 1024


def build_program(cfg):
    D, T, KC, TT, NT, JF, G, DEPTH = cfg.D, cfg.T, cfg.KC, cfg.TT, cfg.NT, cfg.JF, cfg.G, cfg.DEPTH
    NS5, NHG = cfg.NS5, cfg.NHG
    CH = HG_CH
    NCK = T // CH
    nc = bass.Bass("TRN2", target_bir_lowering=False)

    def din(name, shape, dt=F32):
        return nc.dram_tensor(name, list(shape), dt, kind="ExternalInput").ap()

    def dscr(name, shape, dt=F32):
        if DEBUG:
            return nc.dram_tensor(name, list(shape), dt, kind="ExternalOutput").ap()
        return nc.dram_tensor(name, list(shape), dt).ap()

    xT = din("xT", [KC, 128, T])
    gains_d = din("gains", [128, DEPTH * 4 * KC])
    cst_d = din("cst", [128, 5 * 128 + 8 + 1 + 512])
    cmask_d = din("cmask", [128, T])
    lbl_d = din("lbl", [128, DEPTH * KC])
    s5_a1 = din("s5_a1", [NS5, 3, 128, G])
    s5_a2 = din("s5_a2", [NS5, KC, 3, 128, 64])
    s5_b2 = din("s5_b2", [NS5, KC, 2, 128, 64])
    s5_c2 = din("s5_c2", [NS5, KC, 2, 128, 128])
    s5_dd = din("s5_dd", [NS5, 128, KC])
    s5_wg = din("s5_wg", [NS5, KC, 128, KC, 2, 128])
    hg_win = din("hg_win", [NHG, 4 * KC, 128, KC, 128])
    hg_wout = din("hg_wout", [NHG, KC, 128, KC, 128])
    hg_gn = din("hg_gn", [NHG, 128, KC])
    ff_gu = din("ff_gu", [DEPTH, JF, 128, KC, 2, 128])
    ff_dn = din("ff_dn", [DEPTH, KC, 128, JF, 128])
    outT = nc.dram_tensor("outT", [KC, 128, T], F32, kind="ExternalOutput").ap()

    hT = dscr("hT", [KC, 128, T])
    uT = dscr("uT", [KC, 128, T], BF16)
    yT = dscr("yT", [KC, 128, T], BF16)
    qfv = dscr("qfv", [4 * KC, 128, T])
    onT = dscr("onT", [KC, 128, T])
    msc = dscr("msc", [KC, 128, TT])
    wc_gu = [nc.dram_tensor("wc_gu%d" % l, [JF, 128, KC * 2 * 128], BF16).ap() for l in range(DEPTH)]
    wc_dn = [nc.dram_tensor("wc_dn%d" % l, [KC, 128, JF * 128], BF16).ap() for l in range(DEPTH)]
    wc_glu = [nc.dram_tensor("wc_glu%d" % l, [KC, 128, KC * 2 * 128], BF16).ap() for l in range(NS5)]
    wc_win = [nc.dram_tensor("wc_win%d" % l, [4 * KC, 128, KC * 128], BF16).ap() for l in range(NHG)]
    wc_wout = [nc.dram_tensor("wc_wout%d" % l, [KC, 128, KC * 128], BF16).ap() for l in range(NHG)]

    with contextlib.ExitStack() as st:
        cst = st.enter_context(nc.sbuf_tensor("cstsb", [128, 5 * 128 + 8 + 1 + 512], F32))
        gains = st.enter_context(nc.sbuf_tensor("gainsb", [128, DEPTH * 4 * KC], F32))
        cbf = st.enter_context(nc.sbuf_tensor("cbf", [128, 2 * 128], BF16))
        lbt = st.enter_context(nc.sbuf_tensor("lbt", [128, 3 * DEPTH * KC], F32))
        PSF = [st.enter_context(nc.psum_tensor("psf%d" % i, [128, 512], F32)) for i in range(6)]
        PSB = [st.enter_context(nc.psum_tensor("psb%d" % i, [128, 1024], BF16)) for i in range(2)]

        S = Sched(nc)
        S.setup(st)
        uid = [0]
        ident = cst[:, 0:128]
        swapm = cst[:, 128:256]
        maskT = cst[:, 256:384]
        ones32 = cst[:, 384:512]
        mask8 = cst[:, 640:648]
        sgn = cst[:, 648:649]
        iota1 = cst[:, 649:649 + 512]
        identb = cbf[:, 0:128]
        onesb = cbf[:, 128:256]

        PH = [None]

        def phase_begin():
            PH[0] = contextlib.ExitStack()

        def phase_end(final=()):
            S.flush(final_dma_keys=final)
            PH[0].close()
            PH[0] = None

        def A32(n):
            uid[0] += 1
            return PH[0].enter_context(nc.sbuf_tensor("t%d" % uid[0], [128, n], F32))[:]

        def A16(n):
            uid[0] += 1
            return PH[0].enter_context(nc.sbuf_tensor("t%d" % uid[0], [128, n], BF16))[:]

        def AI32(n):
            uid[0] += 1
            return PH[0].enter_context(nc.sbuf_tensor("t%d" % uid[0], [128, n], mybir.dt.int32))[:]

        def key(prefix):
            uid[0] += 1
            return "%s%d" % (prefix, uid[0])

        S.add("sp", lambda e: e.dma_start(out=cst[:], in_=cst_d), writes=["cst"], dma_key="cst")
        S.add("sp", lambda e: e.dma_start(out=gains[:], in_=gains_d), writes=["gains"], dma_key="gains")
        S.add("sp", lambda e: e.dma_start(out=lbt[:, 0:DEPTH * KC], in_=lbl_d), writes=["lbl"], dma_key="lbl")
        S.add("dve", lambda e: e.tensor_copy(out=identb, in_=ident), reads=["cst"], writes=["identb"])
        S.add("dve", lambda e: e.tensor_copy(out=onesb, in_=ones32), reads=["cst"], writes=["onesb"])
        lbe = lbt[:, 0:DEPTH * KC]
        lbv = lbt[:, DEPTH * KC:2 * DEPTH * KC]
        omv = lbt[:, 2 * DEPTH * KC:3 * DEPTH * KC]
        S.add("act", lambda e: e.activation(out=lbe, in_=lbe, func=AF.Exp), reads=["lbl"], writes=["lbl"])
        def _lbsum(e):
            ins = e.tensor_copy(out=omv[:, 0:KC], in_=lbe[:, 0:KC])
            return ins
        S.add("dve", _lbsum, reads=["lbl"], writes=["om0"])
        for l in range(1, DEPTH):
            S.add("dve", lambda e, l=l: e.tensor_tensor(out=omv[:, 0:KC], in0=omv[:, 0:KC], in1=lbe[:, l * KC:(l + 1) * KC], op=ALU.add),
                  reads=["lbl", "om0"], writes=["om0"])
        S.add("dve", lambda e: e.reciprocal(out=omv[:, KC:2 * KC], in_=omv[:, 0:KC]), reads=["om0"], writes=["om1"])
        for l in range(DEPTH):
            S.add("dve", lambda e, l=l: e.tensor_tensor(out=lbe[:, l * KC:(l + 1) * KC], in0=lbe[:, l * KC:(l + 1) * KC],
                                                       in1=omv[:, KC:2 * KC], op=ALU.mult),
                  reads=["lbl", "om1"], writes=["lbl"])
        S.add("dve", lambda e: e.memset(lbv[:, 0:KC], 0.0), writes=["lbv"])
        for l in range(1, DEPTH):
            S.add("dve", lambda e, l=l: e.tensor_tensor(out=lbv[:, l * KC:(l + 1) * KC], in0=lbv[:, (l - 1) * KC:l * KC],
                                                       in1=lbe[:, l * KC:(l + 1) * KC], op=ALU.add),
                  reads=["lbl", "lbv"], writes=["lbv"])
        S.add("dve", lambda e: e.tensor_scalar(out=omv, in0=lbv, scalar1=-1.0, scalar2=1.0, op0=ALU.mult, op1=ALU.add),
              reads=["lbv", "om1"], writes=["omv"])

        def gcol(l, i, kc):
            o = (l * 4 + i) * KC + kc
            return gains[:, o:o + 1]

        def tslice(tt):
            return slice(tt * TT, (tt + 1) * TT)

        class Dense:
            def __init__(self, need_g):
                self.aT = [A16(TT) for _ in range(KC)]
                self.gT = [A16(TT) for _ in range(JF)] if need_g else None
                wsz = max(KC * 2 * 128, JF * 128)
                self.NW = 2
                self.w = [A16(wsz) for _ in range(self.NW)]
                self.wk = ["w%d" % i for i in range(self.NW)]
                self.wi = 0
                self.NH = 4
                self.hs = [A32(TT) for _ in range(self.NH)]
                self.hk = ["hs%d" % i for i in range(self.NH)]
                self.hi = 0
                self.ms = [A32(TT) for _ in range(self.NH)]
                self.mk = ["ms%d" % i for i in range(self.NH)]
                self.mi = 0
                self.sq = [A16(TT) for _ in range(2)]
                self.sqk = ["sq%d" % i for i in range(2)]
                self.sqi = 0
                self.sd = A32(TT)
                self.rstd = A32(TT)
                self.tmp = [A32(TT) for _ in range(2)]
                self.tk = ["tmp%d" % i for i in range(2)]
                self.ti = 0
                self.psi = 0
                self.aTk = "aT"
                self.gTk = "gT"
                self.ssk = "ss"
                self.rk = "rstd"

            def hslot(self):
                i = self.hi % self.NH
                self.hi += 1
                return self.hs[i], self.hk[i]

            def mslot(self):
                i = self.mi % self.NH
                self.mi += 1
                return self.ms[i], self.mk[i]

            def sqslot(self):
                i = self.sqi % 2
                self.sqi += 1
                return self.sq[i], self.sqk[i]

            def tslot(self):
                i = self.ti % 2
                self.ti += 1
                return self.tmp[i], self.tk[i]

            def wslot(self):
                i = self.wi % self.NW
                self.wi += 1
                return self.w[i], self.wk[i]

            def psum(self):
                i = self.psi % 4
                self.psi += 1
                return PSF[i], "psf%d" % i

        PS_SS, PS_SSK = PSF[4], "psf4"

        def load_chunk(Dn, src_ap, src_key):
            slot, k = Dn.hslot()
            S.add("sp", lambda e: e.dma_start(out=slot, in_=src_ap), reads=[src_key], writes=[k], dma_key=k)
            return slot, k

        def ss_accum(Dn, src, srck, first, last):
            sq, sqk = Dn.sqslot()
            S.add("act", lambda e: e.activation(out=sq, in_=src, func=AF.Square), reads=[srck], writes=[sqk])
            S.add("pe", lambda e: e.matmul(PS_SS[:], lhsT=onesb, rhs=sq, start=first, stop=last),
                  reads=[sqk, "onesb"], writes=[PS_SSK])

        def make_rstd(Dn, n_feat):
            S.add("act", lambda e: e.activation(out=Dn.sd, in_=PS_SS[:], func=AF.Sqrt, scale=1.0 / n_feat, bias=EPS),
                  reads=[PS_SSK], writes=[Dn.rk + "sd"])
            S.add("dve", lambda e: e.reciprocal(out=Dn.rstd, in_=Dn.sd), reads=[Dn.rk + "sd"], writes=[Dn.rk])

        def rmsnorm_to_aT(Dn, src, srcname, tt, l, gi):
            for kc in range(KC):
                slot, k = load_chunk(Dn, src[kc][:, tslice(tt)], (srcname, kc))
                ss_accum(Dn, slot, k, kc == 0, kc == KC - 1)
            make_rstd(Dn, D)
            for kc in range(KC):
                slot, k = load_chunk(Dn, src[kc][:, tslice(tt)], (srcname, kc))
                S.add("dve", lambda e, slot=slot, kc=kc: e.scalar_tensor_tensor(
                    out=Dn.aT[kc], in0=slot, scalar=gcol(l, gi, kc), in1=Dn.rstd, op0=ALU.mult, op1=ALU.mult),
                    reads=[k, Dn.rk, "gains"], writes=[(Dn.aTk, kc)])

        def linear(Dn, Wd, nch, kcin, pair, consumer, rhs_of=None, rkeys=None, cache=None, cname=None, first=True):
            if rhs_of is None:
                rhs_of = lambda kc: Dn.aT[kc]
                rkeys = lambda kc: (Dn.aTk, kc)
            for n in range(nch):
                w, wk = Dn.wslot()
                npair = 2 if pair else 1
                wv = w[:, 0:kcin * npair * 128]
                src = Wd[n]
                if pair:
                    src = src.rearrange("p k a c -> p (k a c)")
                    wv3 = wv.rearrange("p (k a c) -> p k a c", a=2, c=128)
                else:
                    src = src.rearrange("p k c -> p (k c)")
                    wv3 = wv.rearrange("p (k c) -> p k c", c=128)
                if cache is None or first:
                    S.add("pool", lambda e, wv=wv, src=src: e.dma_start(out=wv, in_=src, max_dma_last_dim=8192),
                          writes=[wk], dma_key=wk)
                    if cache is not None:
                        S.add("sp", lambda e, wv=wv, n=n: e.dma_start(out=cache[n], in_=wv), reads=[wk], writes=[(cname, n)],
                              dma_key=wk + "st")
                else:
                    S.add("pool", lambda e, wv=wv, n=n: e.dma_start(out=wv, in_=cache[n]), reads=[(cname, n)], writes=[wk],
                          dma_key=wk)
                outs = []
                for a in range(npair):
                    ps, pk = Dn.psum()
                    for kc in range(kcin):
                        lhsT = wv3[:, kc, a, :] if pair else wv3[:, kc, :]
                        S.add("pe", lambda e, ps=ps, lhsT=lhsT, kc=kc: e.matmul(ps[:], lhsT=lhsT, rhs=rhs_of(kc),
                                                                               start=(kc == 0), stop=(kc == kcin - 1)),
                              reads=[wk, rkeys(kc)], writes=[pk])
                    outs.append((ps, pk))
                consumer(n, outs)

        def out_chunk(Dn, n, nlast, src, srck, already_sbuf=False):
            if already_sbuf:
                ms, mk = src, srck
            else:
                ms, mk = Dn.mslot()
                S.add("act", lambda e: e.activation(out=ms, in_=src, func=AF.Copy), reads=[srck], writes=[mk])
            ss_accum(Dn, ms, mk, n == 0, n == nlast)
            S.add("sp", lambda e: e.dma_start(out=msc[n], in_=ms), reads=[mk], writes=[("msc", n)], dma_key=mk + "st")

        def resid_update(Dn, tt, l, gi, dst, dstname):
            make_rstd(Dn, D)
            for kc in range(KC):
                ms, mk = Dn.mslot()
                S.add("sp", lambda e, ms=ms, kc=kc: e.dma_start(out=ms, in_=msc[kc]), reads=[("msc", kc)], writes=[mk],
                      dma_key=mk)
                hs, hk = load_chunk(Dn, hT[kc][:, tslice(tt)], ("hT", kc))
                S.add("dve", lambda e, ms=ms, kc=kc: e.scalar_tensor_tensor(
                    out=ms, in0=ms, scalar=gcol(l, gi, kc), in1=Dn.rstd, op0=ALU.mult, op1=ALU.mult),
                    reads=[mk, Dn.rk, "gains"], writes=[mk])
                S.add("pool", lambda e, ms=ms, hs=hs: e.tensor_tensor(out=hs, in0=hs, in1=ms, op=ALU.add),
                      reads=[mk, hk], writes=[hk])
                S.add("sp", lambda e, hs=hs, kc=kc: e.dma_start(out=dst[kc][:, tslice(tt)], in_=hs), reads=[hk],
                      writes=[(dstname, kc)], dma_key=hk + "st")

        def ffn_tile(Dn, tt, l, last_layer):
            rmsnorm_to_aT(Dn, hT, "hT", tt, l, 2)

            def gu_cons(j, outs):
                (pg, pgk), (pu, puk) = outs
                t, tk = Dn.tslot()
                S.add("act", lambda e: e.activation(out=t, in_=pg[:], func=AF.Silu), reads=[pgk], writes=[tk])
                S.add("dve", lambda e: e.tensor_tensor(out=Dn.gT[j], in0=pu[:], in1=t, op=ALU.mult),
                      reads=[puk, tk], writes=[(Dn.gTk, j)])
            linear(Dn, ff_gu[l], JF, KC, True, gu_cons, cache=wc_gu[l], cname="wc_gu", first=(tt == 0))

            def dn_cons(m, outs):
                (pf, pfk), = outs
                out_chunk(Dn, m, KC - 1, pf[:], pfk)
            linear(Dn, ff_dn[l], KC, JF, False, dn_cons, rhs_of=lambda j: Dn.gT[j], rkeys=lambda j: (Dn.gTk, j),
                   cache=wc_dn[l], cname="wc_dn", first=(tt == 0))
            if last_layer:
                resid_update(Dn, tt, l, 3, outT, "outT")
            else:
                resid_update(Dn, tt, l, 3, hT, "hT")

        S.flush()
        phase_begin()
        cp = [A32(T) for _ in range(2)]
        cpk = ["cp0", "cp1"]
        for kc in range(KC):
            i = kc % 2
            S.add("sp", lambda e, i=i, kc=kc: e.dma_start(out=cp[i], in_=xT[kc]), writes=[cpk[i]], dma_key=cpk[i])
            S.add("sp", lambda e, i=i, kc=kc: e.dma_start(out=hT[kc], in_=cp[i]), reads=[cpk[i]], writes=[("hT", kc)],
                  dma_key=cpk[i] + "st")
        phase_end()

        def s5_phase(j):
            TB = 512
            NB = T // TB
            phase_begin()
            a1 = A32(3 * G).rearrange("p (a g) -> p a g", g=G)
            th1 = A32(G)
            r1 = A32(G)
            cosL = A32(G)
            ssinL = A32(G)
            t1g = A32(G)
            ddc = A32(KC)
            S.add("sp", lambda e: e.dma_start(out=a1, in_=s5_a1[j].rearrange("a p g -> p a g")), writes=["a1"], dma_key="a1")
            S.add("sp", lambda e: e.dma_start(out=ddc, in_=s5_dd[j]), writes=["ddc"], dma_key="ddc")
            S.add("act", lambda e: e.activation(out=a1[:, 2, :], in_=a1[:, 2, :], func=AF.Exp), reads=["a1"], writes=["a1"])
            S.add("dve", lambda e: e.tensor_tensor(out=th1, in0=a1[:, 1, :], in1=a1[:, 2, :], op=ALU.mult), reads=["a1"], writes=["th1"])
            S.add("dve", lambda e: e.scalar_tensor_tensor(out=r1, in0=a1[:, 0, :], scalar=-1e-4, in1=a1[:, 2, :], op0=ALU.min, op1=ALU.mult),
                  reads=["a1"], writes=["r1"])
            S.add("act", lambda e: e.activation(out=r1, in_=r1, func=AF.Exp), reads=["r1"], writes=["r1"])
            rr_f = A32(8 * 512)
            rr_i = AI32(8 * 512)

            def sin_of(out, x, n, rkeys, wkeys):
                f_ = rr_f[:, 0:n]
                i_ = rr_i[:, 0:n]
                RR = "rr"
                S.add("dve", lambda e: e.tensor_scalar(out=f_, in0=x, scalar1=1.0 / (2 * PI), scalar2=None, op0=ALU.mult),
                      reads=list(rkeys) + [RR], writes=[RR])
                S.add("dve", lambda e: e.tensor_copy(out=i_, in_=f_), reads=[RR], writes=[RR + "i"])
                S.add("dve", lambda e: e.tensor_copy(out=f_, in_=i_), reads=[RR + "i"], writes=[RR])
                S.add("dve", lambda e: e.scalar_tensor_tensor(out=x, in0=f_, scalar=-2 * PI, in1=x, op0=ALU.mult, op1=ALU.add),
                      reads=[RR] + list(rkeys), writes=list(rkeys))
                S.add("dve", lambda e: e.tensor_scalar(out=f_, in0=x, scalar1=PI, scalar2=2 * PI, op0=ALU.is_gt, op1=ALU.mult),
                      reads=list(rkeys) + [RR], writes=[RR])
                S.add("dve", lambda e: e.tensor_tensor(out=x, in0=x, in1=f_, op=ALU.subtract), reads=[RR] + list(rkeys), writes=list(rkeys))
                S.add("dve", lambda e: e.tensor_scalar(out=f_, in0=x, scalar1=-PI, scalar2=2 * PI, op0=ALU.is_lt, op1=ALU.mult),
                      reads=list(rkeys) + [RR], writes=[RR])
                S.add("dve", lambda e: e.tensor_tensor(out=x, in0=x, in1=f_, op=ALU.add), reads=[RR] + list(rkeys), writes=list(rkeys))
                S.add("act", lambda e: e.activation(out=out, in_=x, func=AF.Sin), reads=list(rkeys) + list(wkeys), writes=list(wkeys))

            S.add("dve", lambda e: e.tensor_scalar(out=t1g, in0=th1, scalar1=float(TB), scalar2=None, op0=ALU.mult),
                  reads=["th1"], writes=["t1g"])
            sin_of(ssinL, t1g, G, ["t1g"], ["ssinL"])
            S.add("dve", lambda e: e.tensor_scalar(out=ssinL, in0=ssinL, scalar1=sgn, scalar2=None, op0=ALU.mult),
                  reads=["ssinL", "cst"], writes=["ssinL"])
            S.add("dve", lambda e: e.tensor_scalar(out=t1g, in0=th1, scalar1=float(TB), scalar2=0.5 * PI, op0=ALU.mult, op1=ALU.add),
                  reads=["th1", "ssinL", "t1g"], writes=["t1g"])
            sin_of(cosL, t1g, G, ["t1g"], ["cosL"])

            u = [A16(T) for _ in range(2)]
            uk = ["u0", "u1"]
            yb = [A16(T) for _ in range(2)]
            ybk = ["yb0", "yb1"]
            a2 = A32(3 * 64).rearrange("p (a q) -> p a q", q=64)
            b2 = A32(2 * 64).rearrange("p (a q) -> p a q", q=64)
            c2 = A32(2 * 128).rearrange("p (a q) -> p a q", q=128)
            wk_ = [A32(64) for _ in range(12)]
            Bf = A32(128)
            Bf2 = A32(128)
            CA = A32(128)
            CB = A32(128)
            ZB = A16(8 * 128).rearrange("p (g c) -> p g c", c=128)
            ZB2 = A16(8 * 128).rearrange("p (g c) -> p g c", c=128)
            ZCA = A16(8 * 128).rearrange("p (g c) -> p g c", c=128)
            ZCB = A16(8 * 128).rearrange("p (g c) -> p g c", c=128)
            ROT = A32(8 * 128).rearrange("p (g c) -> p g c", c=128)
            COS = A32(8 * TB).rearrange("p (g t) -> p g t", t=TB)
            SIN = A32(8 * TB).rearrange("p (g t) -> p g t", t=TB)
            RT = A32(8 * TB).rearrange("p (g t) -> p g t", t=TB)
            targ = A32(8 * TB)
            targ3 = targ.rearrange("p (g t) -> p g t", t=TB)
            COSf = COS.rearrange("p g t -> p (g t)")
            SINf = SIN.rearrange("p g t -> p (g t)")
            st8 = A32(8)
            t1 = [A32(TB) for _ in range(2)]
            t2 = [A32(TB) for _ in range(2)]
            vt = [A32(TB) for _ in range(2)]
            vv = [A32(TB) for _ in range(2)]
            w1 = [A16(TB) for _ in range(2)]
            w2 = [A16(TB) for _ in range(2)]
            yy = A32(TB)
            gsq = A32(TB)
            gin = A32(TB)
            bk = [["s5b%d_%d" % (a_, b_) for b_ in range(8)] for a_ in range(2)]
            PS1 = [(PSF[0], "psf0"), (PSF[1], "psf1")]
            PS2 = [(PSF[2], "psf2"), (PSF[3], "psf3")]
            PSY = [(PSF[4], "psf4"), (PSF[5], "psf5")]
            PST = (PSB[0], "psb0")
            pst32 = PSB[0][:].bitcast(F32)

            it = 0
            for c8 in range(KC):
                ui = c8 % 2
                S.add("sp", lambda e, ui=ui, c8=c8: e.dma_start(out=u[ui], in_=uT[c8]), reads=[("uT", c8)], writes=[uk[ui]],
                      dma_key=uk[ui])
                S.add("sp", lambda e, c8=c8: e.dma_start(out=a2, in_=s5_a2[j, c8].rearrange("a p q -> p a q")), writes=["a2"], dma_key="a2")
                S.add("sp", lambda e, c8=c8: e.dma_start(out=b2, in_=s5_b2[j, c8].rearrange("a p q -> p a q")), writes=["b2"], dma_key="b2")
                S.add("sp", lambda e, c8=c8: e.dma_start(out=c2, in_=s5_c2[j, c8].rearrange("a p q -> p a q")), writes=["c2"], dma_key="c2")
                lam_re, dt_, x1, mag, th, ang, sn, cs, den, ere, zre, zim = wk_
                P = "s5p"
                def V(fn, eng="dve", r=(P,), w=(P,)):
                    S.add(eng, fn, reads=list(r), writes=list(w))
                V(lambda e: e.tensor_scalar(out=lam_re, in0=a2[:, 0, :], scalar1=-1e-4, scalar2=None, op0=ALU.min), r=("a2", P))
                V(lambda e: e.activation(out=dt_, in_=a2[:, 2, :], func=AF.Exp), eng="act", r=("a2", P))
                V(lambda e: e.tensor_tensor(out=x1, in0=lam_re, in1=dt_, op=ALU.mult))
                V(lambda e: e.activation(out=mag, in_=x1, func=AF.Exp), eng="act")
                V(lambda e: e.tensor_tensor(out=th, in0=a2[:, 1, :], in1=dt_, op=ALU.mult), r=("a2", P))
                V(lambda e: e.tensor_copy(out=ang, in_=th))
                sin_of(sn, ang, 64, [P], [P])
                V(lambda e: e.tensor_scalar(out=ang, in0=th, scalar1=0.5 * PI, scalar2=None, op0=ALU.add))
                sin_of(cs, ang, 64, [P], [P])
                V(lambda e: e.tensor_tensor(out=sn, in0=sn, in1=mag, op=ALU.mult))
                V(lambda e: e.tensor_tensor(out=cs, in0=cs, in1=mag, op=ALU.mult))
                V(lambda e: e.tensor_tensor(out=den, in0=lam_re, in1=lam_re, op=ALU.mult))
                V(lambda e: e.tensor_tensor(out=x1, in0=a2[:, 1, :], in1=a2[:, 1, :], op=ALU.mult), r=("a2", P))
                V(lambda e: e.tensor_tensor(out=den, in0=den, in1=x1, op=ALU.add))
                V(lambda e: e.reciprocal(out=den, in_=den))
                V(lambda e: e.tensor_scalar(out=ere, in0=cs, scalar1=-1.0, scalar2=None, op0=ALU.add))
                V(lambda e: e.tensor_tensor(out=zre, in0=ere, in1=lam_re, op=ALU.mult))
                V(lambda e: e.tensor_tensor(out=x1, in0=sn, in1=a2[:, 1, :], op=ALU.mult), r=("a2", P))
                V(lambda e: e.tensor_tensor(out=zre, in0=zre, in1=x1, op=ALU.add))
                V(lambda e: e.tensor_tensor(out=zre, in0=zre, in1=den, op=ALU.mult))
                V(lambda e: e.tensor_tensor(out=zim, in0=sn, in1=lam_re, op=ALU.mult))
                V(lambda e: e.tensor_tensor(out=x1, in0=ere, in1=a2[:, 1, :], op=ALU.mult), r=("a2", P))
                V(lambda e: e.tensor_tensor(out=zim, in0=zim, in1=x1, op=ALU.subtract))
                V(lambda e: e.tensor_tensor(out=zim, in0=zim, in1=den, op=ALU.mult))
                V(lambda e: e.tensor_tensor(out=Bf[:, 0:64], in0=zre, in1=b2[:, 0, :], op=ALU.mult), r=("b2", P), w=("Bf",))
                V(lambda e: e.tensor_tensor(out=x1, in0=zim, in1=b2[:, 1, :], op=ALU.mult), r=("b2", P))
                V(lambda e: e.tensor_tensor(out=Bf[:, 0:64], in0=Bf[:, 0:64], in1=x1, op=ALU.subtract), r=("Bf", P), w=("Bf",))
                V(lambda e: e.tensor_tensor(out=Bf[:, 64:128], in0=zre, in1=b2[:, 1, :], op=ALU.mult), r=("b2", P, "Bf"), w=("Bf",))
                V(lambda e: e.tensor_tensor(out=x1, in0=zim, in1=b2[:, 0, :], op=ALU.mult), r=("b2", P, "Bf"))
                V(lambda e: e.tensor_tensor(out=Bf[:, 64:128], in0=Bf[:, 64:128], in1=x1, op=ALU.add), r=("Bf", P), w=("Bf",))
                V(lambda e: e.tensor_copy(out=Bf2[:, 0:64], in_=Bf[:, 64:128]), r=("Bf",), w=("Bf2",))
                V(lambda e: e.tensor_scalar(out=Bf2[:, 64:128], in0=Bf[:, 0:64], scalar1=-1.0, scalar2=None, op0=ALU.mult), r=("Bf", "Bf2"), w=("Bf2",))
                V(lambda e: e.tensor_scalar(out=CA, in0=c2[:, 0, :], scalar1=sgn, scalar2=None, op0=ALU.mult), r=("c2", "cst"), w=("CA",))
                V(lambda e: e.tensor_scalar(out=CB, in0=c2[:, 1, :], scalar1=-1.0, scalar2=None, op0=ALU.mult), r=("c2",), w=("CB",))
                for g_ in range(8):
                    g = c8 * 8 + g_
                    V(lambda e, g_=g_: e.tensor_scalar(out=ZB[:, g_, :], in0=Bf, scalar1=mask8[:, g_:g_ + 1], scalar2=None, op0=ALU.mult),
                      r=("Bf", "cst"), w=("Z",))
                    V(lambda e, g_=g_: e.tensor_scalar(out=ZB2[:, g_, :], in0=Bf2, scalar1=mask8[:, g_:g_ + 1], scalar2=None, op0=ALU.mult),
                      r=("Bf2", "cst", "Z"), w=("Z",))
                S.add("pool", lambda e: e.memset(ZCA, 0.0), reads=["Z"], writes=["ZC"])
                S.add("pool", lambda e: e.memset(ZCB, 0.0), reads=["ZC"], writes=["ZC"])
                for g_ in range(8):
                    g = c8 * 8 + g_
                    cs_ = slice(g_ * 16, g_ * 16 + 16)
                    V(lambda e, g_=g_, cs_=cs_: e.tensor_copy(out=ZCA[:, g_, cs_], in_=CA[:, cs_]), r=("CA", "ZC"), w=("ZC",))
                    V(lambda e, g_=g_, cs_=cs_: e.tensor_copy(out=ZCB[:, g_, cs_], in_=CB[:, cs_]), r=("CB", "ZC"), w=("ZC",))
                    V(lambda e, g_=g_, g=g: e.tensor_scalar(out=ROT[:, g_, :], in0=ident, scalar1=cosL[:, g:g + 1], scalar2=None, op0=ALU.mult),
                      r=("cst", "cosL", "ROT"), w=("ROT",))
                    V(lambda e, g_=g_, g=g: e.scalar_tensor_tensor(out=ROT[:, g_, :], in0=swapm, scalar=ssinL[:, g:g + 1], in1=ROT[:, g_, :],
                                                                  op0=ALU.mult, op1=ALU.add),
                      r=("cst", "ssinL", "ROT"), w=("ROT",))
                g0 = c8 * 8
                iota_b = iota1.unsqueeze(1).to_broadcast([128, 8, TB])
                V(lambda e, g0=g0: e.tensor_tensor(out=targ3, in0=iota_b, in1=th1[:, g0:g0 + 8].unsqueeze(2).to_broadcast([128, 8, TB]), op=ALU.mult),
                  r=("cst", "th1", "targ", "TAB"), w=("targ",))
                sin_of(SINf, targ, 8 * TB, ["targ"], ["TAB"])
                V(lambda e, g0=g0: e.tensor_tensor(out=targ3, in0=iota_b, in1=th1[:, g0:g0 + 8].unsqueeze(2).to_broadcast([128, 8, TB]), op=ALU.mult),
                  r=("cst", "th1", "targ", "TAB"), w=("targ",))
                V(lambda e: e.tensor_scalar(out=targ, in0=targ, scalar1=0.5 * PI, scalar2=None, op0=ALU.add), r=("targ",), w=("targ",))
                sin_of(COSf, targ, 8 * TB, ["targ"], ["TAB"])
                for g_ in range(8):
                    g = c8 * 8 + g_
                    V(lambda e, g_=g_, g=g: e.tensor_scalar(out=RT[:, g_, :], in0=iota1, scalar1=0.0, scalar2=r1[:, g:g + 1],
                                                            op0=ALU.mult, op1=ALU.add),
                      eng="pool", r=("cst", "r1", "TAB"), w=("TAB",))
                yi = c8 % 2
                for tb in range(NB):
                    tsl = slice(tb * TB, (tb + 1) * TB)
                    psy, psyk = PSY[tb % 2]
                    for g_ in range(8):
                        b = it % 2
                        it += 1
                        p1, p1k = PS1[b]
                        p2, p2k = PS2[b]
                        kb = bk[b]
                        S.add("pe", lambda e, p1=p1, g_=g_, tsl=tsl, ui=ui: e.matmul(p1[:], lhsT=ZB[:, g_, :], rhs=u[ui][:, tsl], start=True, stop=True),
                              reads=["Z", uk[ui]], writes=[p1k])
                        S.add("pe", lambda e, p2=p2, g_=g_, tsl=tsl, ui=ui: e.matmul(p2[:], lhsT=ZB2[:, g_, :], rhs=u[ui][:, tsl], start=True, stop=True),
                              reads=["Z", uk[ui]], writes=[p2k])
                        S.add("dve", lambda e, b=b, p1=p1, g_=g_: e.tensor_tensor(out=t1[b], in0=p1[:], in1=COS[:, g_, :], op=ALU.mult),
                              reads=[p1k, "TAB"], writes=[kb[0]])
                        S.add("dve", lambda e, b=b, p2=p2, g_=g_: e.tensor_tensor(out=t2[b], in0=p2[:], in1=SIN[:, g_, :], op=ALU.mult),
                              reads=[p2k, "TAB"], writes=[kb[1]])
                        S.add("pool", lambda e, b=b: e.tensor_tensor(out=vt[b], in0=t1[b], in1=t2[b], op=ALU.add),
                              reads=[kb[0], kb[1]], writes=[kb[2]])
                        init = 0.0 if tb == 0 else st8[:, g_:g_ + 1]
                        S.add("dve", lambda e, b=b, g_=g_, init=init: e.tensor_tensor_scan(out=vv[b], data0=RT[:, g_, :], data1=vt[b], initial=init,
                                                                                         op0=ALU.mult, op1=ALU.add),
                              reads=[kb[2], "TAB", ("st8", g_)], writes=[kb[3]])
                        if tb < NB - 1:
                            S.add("pe", lambda e, b=b, g_=g_: e.matmul(pst32[:, g_:g_ + 1], lhsT=ROT[:, g_, :], rhs=vv[b][:, TB - 1:TB], start=True, stop=True),
                                  reads=[kb[3], "ROT"], writes=[("pst", g_)])
                            S.add("act", lambda e, g_=g_: e.activation(out=st8[:, g_:g_ + 1], in_=pst32[:, g_:g_ + 1], func=AF.Copy),
                                  reads=[("pst", g_)], writes=[("st8", g_)])
                        S.add("pool", lambda e, b=b, g_=g_: e.tensor_tensor(out=w1[b], in0=vv[b], in1=COS[:, g_, :], op=ALU.mult),
                              reads=[kb[3], "TAB"], writes=[kb[4]])
                        S.add("pool", lambda e, b=b, g_=g_: e.tensor_tensor(out=w2[b], in0=vv[b], in1=SIN[:, g_, :], op=ALU.mult),
                              reads=[kb[3], "TAB"], writes=[kb[5]])
                        S.add("pe", lambda e, b=b, g_=g_, psy=psy: e.matmul(psy[:], lhsT=ZCA[:, g_, :], rhs=w1[b], start=(g_ == 0), stop=False),
                              reads=[kb[4], "ZC"], writes=[psyk])
                        S.add("pe", lambda e, b=b, g_=g_, psy=psy: e.matmul(psy[:], lhsT=ZCB[:, g_, :], rhs=w2[b], start=False, stop=(g_ == 7)),
                              reads=[kb[5], "ZC"], writes=[psyk])
                    S.add("dve", lambda e, psy=psy, tsl=tsl, c8=c8, ui=ui: e.scalar_tensor_tensor(out=yy, in0=u[ui][:, tsl], scalar=ddc[:, c8:c8 + 1], in1=psy[:],
                                                                                         op0=ALU.mult, op1=ALU.add),
                          reads=[psyk, uk[ui], "ddc", "yy"], writes=["yy"])
                    S.add("act", lambda e: e.activation(out=gsq, in_=yy, func=AF.Square), reads=["yy", "gsq"], writes=["gsq"])
                    S.add("dve", lambda e: e.tensor_scalar(out=gsq, in0=gsq, scalar1=0.044715, scalar2=1.0, op0=ALU.mult, op1=ALU.add),
                          reads=["gsq"], writes=["gsq"])
                    S.add("dve", lambda e: e.tensor_tensor(out=gin, in0=gsq, in1=yy, op=ALU.mult), reads=["gsq", "yy", "gin"], writes=["gin"])
                    S.add("act", lambda e: e.activation(out=gin, in_=gin, func=AF.Sigmoid, scale=1.5957691216057308), reads=["gin"], writes=["gin"])
                    S.add("pool", lambda e, tsl=tsl, yi=yi: e.tensor_tensor(out=yb[yi][:, tsl], in0=yy, in1=gin, op=ALU.mult),
                          reads=["gin", "yy"], writes=[ybk[yi]])
                S.add("sp", lambda e, c8=c8, yi=yi: e.dma_start(out=yT[c8], in_=yb[yi]), reads=[ybk[yi]], writes=[("yT", c8)], dma_key=ybk[yi] + "st")
            phase_end()

        def hgrn_phase(jh, l):
            phase_begin()
            CM = A16(T)
            S.add("pool", lambda e: e.dma_start(out=CM, in_=cmask_d, max_dma_last_dim=8192), writes=["CM"], dma_key="CM")
            qr = A32(T)
            fr = A32(T)
            vr = A32(T)
            lf = A32(T)
            cum = A32(T)
            kk = A32(T)
            qt = A16(T)
            kt = A16(T)
            vb = A16(T)
            osb = A32(T)
            ecl = A32(NCK)
            ecm = A32(NCK + 1)
            ecd = A32(NCK)
            PTs = [A16(CH) for _ in range(2)]
            vtok = [A16(128) for _ in range(2)]
            ktok = [A16(128) for _ in range(2)]
            Sst = A32(128)
            Sbf = A16(128)
            stmp = A32(128)
            osq = [A16(512) for _ in range(2)]
            sdh = A32(512)
            rsh = A32(512)
            gn = A32(KC)
            S.add("sp", lambda e: e.dma_start(out=gn, in_=hg_gn[jh]), writes=["gn"], dma_key="gn")
            cum3 = cum.rearrange("p (c t) -> p c t", t=CH)
            lf3 = lf.rearrange("p (c t) -> p c t", t=CH)
            for hd in range(KC):
                lo = l * KC + hd
                H = "hg"
                S.add("sp", lambda e, hd=hd: e.dma_start(out=qr, in_=qfv[hd]), reads=[("qfv", hd)], writes=["qr"], dma_key="qr")
                S.add("sp", lambda e, hd=hd: e.dma_start(out=fr, in_=qfv[KC + hd]), reads=[("qfv", KC + hd)], writes=["fr"], dma_key="fr")
                S.add("sp", lambda e, hd=hd: e.dma_start(out=vr, in_=qfv[2 * KC + hd]), reads=[("qfv", 2 * KC + hd)], writes=["vr"], dma_key="vr")
                S.add("act", lambda e: e.activation(out=qr, in_=qr, func=AF.Silu), reads=["qr"], writes=["qr"])
                S.add("act", lambda e: e.activation(out=fr, in_=fr, func=AF.Sigmoid), reads=["fr"], writes=["fr"])
                S.add("dve", lambda e, lo=lo: e.tensor_scalar(out=fr, in0=fr, scalar1=omv[:, lo:lo + 1], scalar2=lbv[:, lo:lo + 1], op0=ALU.mult, op1=ALU.add),
                      reads=["fr", "omv", "lbv"], writes=["fr"])
                S.add("act", lambda e: e.activation(out=lf, in_=fr, func=AF.Ln), reads=["fr", "lf"], writes=["lf"])
                S.add("dve", lambda e: e.tensor_scalar(out=kk, in0=fr, scalar1=-1.0, scalar2=1.0, op0=ALU.mult, op1=ALU.add),
                      reads=["fr", "kk"], writes=["kk"])
                S.add("dve", lambda e: e.tensor_tensor_scan(out=cum, data0=CM, data1=lf, initial=0.0, op0=ALU.mult, op1=ALU.add),
                      reads=["CM", "lf", "cum"], writes=["cum"])
                cmid = cum3[:, :, CH // 2 - 1]
                clast = cum3[:, :, CH - 1]
                S.add("act", lambda e: e.activation(out=ecl, in_=clast, func=AF.Exp), reads=["cum", "ecl"], writes=["ecl"])
                S.add("act", lambda e: e.activation(out=ecm[:, 0:NCK], in_=cmid, func=AF.Exp), reads=["cum", "ecm"], writes=["ecm"])
                S.add("dve", lambda e: e.tensor_tensor(out=ecd, in0=clast, in1=cmid, op=ALU.subtract), reads=["cum", "ecd"], writes=["ecd"])
                S.add("act", lambda e: e.activation(out=ecd, in_=ecd, func=AF.Exp), reads=["ecd"], writes=["ecd"])
                S.add("dve", lambda e: e.tensor_tensor(out=lf3, in0=cum3, in1=cum3[:, :, CH // 2 - 1:CH // 2].to_broadcast([128, NCK, CH]),
                                                       op=ALU.subtract),
                      reads=["cum", "lf"], writes=["lf"])
                S.add("act", lambda e: e.activation(out=cum, in_=lf, func=AF.Exp), reads=["lf", "cum", "ecl", "ecm", "ecd"], writes=["cum"])
                S.add("act", lambda e: e.activation(out=lf, in_=lf, func=AF.Exp, scale=-1.0), reads=["lf", "cum"], writes=["lf"])
                S.add("dve", lambda e: e.tensor_tensor(out=qt, in0=qr, in1=cum, op=ALU.mult), reads=["qr", "cum", "qt"], writes=["qt"])
                S.add("pool", lambda e: e.tensor_tensor(out=kt, in0=kk, in1=lf, op=ALU.mult), reads=["kk", "lf", "kt"], writes=["kt"])
                S.add("act", lambda e: e.activation(out=vb, in_=vr, func=AF.Copy), reads=["vr", "vb"], writes=["vb"])
                for ch in range(NCK):
                    csl = slice(ch * CH, (ch + 1) * CH)
                    b = ch % 2
                    psS, psSk = PSF[b], "psf%d" % b
                    psO, psOk = PSF[2 + b], "psf%d" % (2 + b)
                    psT, psTk = PSF[4], "psf4"
                    pbv = PSB[0][0:CH, b * 128:(b + 1) * 128]
                    pbk = PSB[1][0:CH, b * 128:(b + 1) * 128]
                    S.add("pe", lambda e, psS=psS, csl=csl: e.matmul(psS[0:CH, 0:CH], lhsT=kt[:, csl], rhs=qt[:, csl], start=True, stop=True),
                          reads=["kt", "qt"], writes=[psSk])
                    S.add("dve", lambda e, psS=psS, b=b: e.tensor_tensor(out=PTs[b][0:CH, :], in0=psS[0:CH, 0:CH], in1=maskT[0:CH, 0:CH], op=ALU.mult),
                          reads=[psSk, "cst", ("PT", b)], writes=[("PT", b)])
                    S.add("pe", lambda e, pbv=pbv, csl=csl: e.transpose(pbv, vb[:, csl], identb), reads=["vb", "identb"], writes=[("pbv", b)])
                    S.add("act", lambda e, pbv=pbv, b=b: e.activation(out=vtok[b][0:CH, :], in_=pbv, func=AF.Copy), reads=[("pbv", b), ("vtok", b)], writes=[("vtok", b)])
                    S.add("pe", lambda e, pbk=pbk, csl=csl: e.transpose(pbk, kt[:, csl], identb), reads=["kt", "identb"], writes=[("pbk", b)])
                    S.add("act", lambda e, pbk=pbk, b=b: e.activation(out=ktok[b][0:CH, :], in_=pbk, func=AF.Copy), reads=[("pbk", b), ("ktok", b)], writes=[("ktok", b)])
                    S.add("pe", lambda e, psO=psO, b=b, ch=ch: e.matmul(psO[:, 0:CH], lhsT=vtok[b][0:CH, :], rhs=PTs[b][0:CH, :], start=True, stop=(ch == 0)),
                          reads=[("vtok", b), ("PT", b)], writes=[psOk])
                    if ch > 0:
                        S.add("pe", lambda e, psO=psO, csl=csl: e.matmul(psO[:, 0:CH], lhsT=Sbf, rhs=qt[:, csl], start=False, stop=True),
                              reads=["Sbf", "qt"], writes=[psOk])
                    S.add("act", lambda e, psO=psO, csl=csl: e.activation(out=osb[:, csl], in_=psO[:, 0:CH], func=AF.Copy),
                          reads=[psOk, "osb"], writes=["osb"])
                    if ch < NCK - 1:
                        S.add("pe", lambda e, psT=psT, b=b: e.matmul(psT[:, 0:128], lhsT=ktok[b][0:CH, :], rhs=vtok[b][0:CH, :], start=True, stop=True),
                              reads=[("ktok", b), ("vtok", b)], writes=[psTk])
                        if ch == 0:
                            S.add("dve", lambda e, psT=psT, ch=ch: e.tensor_scalar(out=Sst, in0=psT[:, 0:128], scalar1=ecd[:, ch:ch + 1], scalar2=None, op0=ALU.mult),
                                  reads=[psTk, "ecd", "Sst"], writes=["Sst"])
                        else:
                            S.add("dve", lambda e, psT=psT, ch=ch: e.tensor_scalar(out=stmp, in0=psT[:, 0:128], scalar1=ecd[:, ch:ch + 1], scalar2=None, op0=ALU.mult),
                                  reads=[psTk, "ecd", "stmp"], writes=["stmp"])
                            S.add("dve", lambda e, ch=ch: e.scalar_tensor_tensor(out=Sst, in0=Sst, scalar=ecl[:, ch:ch + 1], in1=stmp, op0=ALU.mult, op1=ALU.add),
                                  reads=["stmp", "ecl", "Sst"], writes=["Sst"])
                        S.add("dve", lambda e, ch=ch: e.tensor_scalar(out=Sbf, in0=Sst, scalar1=ecm[:, ch + 1:ch + 2], scalar2=None, op0=ALU.mult),
                              reads=["Sst", "ecm", "Sbf"], writes=["Sbf"])
                for tb in range(T // 512):
                    tsl = slice(tb * 512, (tb + 1) * 512)
                    b = tb % 2
                    S.add("act", lambda e, b=b, tsl=tsl: e.activation(out=osq[b], in_=osb[:, tsl], func=AF.Square), reads=["osb", ("osq", b)], writes=[("osq", b)])
                    S.add("pe", lambda e, b=b: e.matmul(PSF[5][:], lhsT=onesb, rhs=osq[b], start=True, stop=True), reads=[("osq", b), "onesb"], writes=["psf5"])
                    S.add("act", lambda e: e.activation(out=sdh, in_=PSF[5][:], func=AF.Sqrt, scale=1.0 / 128, bias=EPS), reads=["psf5", "sdh"], writes=["sdh"])
                    S.add("dve", lambda e: e.reciprocal(out=rsh, in_=sdh), reads=["sdh", "rsh"], writes=["rsh"])
                    S.add("dve", lambda e, tsl=tsl, hd=hd: e.scalar_tensor_tensor(out=osb[:, tsl], in0=osb[:, tsl], scalar=gn[:, hd:hd + 1], in1=rsh,
                                                                                 op0=ALU.mult, op1=ALU.mult),
                          reads=["rsh", "gn", "osb"], writes=["osb"])
                S.add("sp", lambda e, hd=hd: e.dma_start(out=onT[hd], in_=osb), reads=["osb"], writes=[("onT", hd)], dma_key="osbst")
            phase_end()

        for l in range(DEPTH if KSTOP < 0 else KSTOP):
            j = l // 2
            is_s5 = (l % 2 == 0)
            phase_begin()
            Dn = Dense(need_g=False)
            for tt in range(NT):
                rmsnorm_to_aT(Dn, hT, "hT", tt, l, 0)
                if is_s5:
                    for kc in range(KC):
                        S.add("sp", lambda e, kc=kc, tt=tt, Dn=Dn: e.dma_start(out=uT[kc][:, tslice(tt)], in_=Dn.aT[kc]),
                              reads=[(Dn.aTk, kc)], writes=[("uT", kc)], dma_key=Dn.aTk + "st%d" % (kc % 4))
                else:
                    def win_cons(n, outs, tt=tt):
                        (ps, pk), = outs
                        ms, mk = Dn.mslot()
                        S.add("act", lambda e: e.activation(out=ms, in_=ps[:], func=AF.Copy), reads=[pk], writes=[mk])
                        S.add("sp", lambda e: e.dma_start(out=qfv[n][:, tslice(tt)], in_=ms), reads=[mk], writes=[("qfv", n)], dma_key=mk + "st")
                    linear(Dn, hg_win[j], 4 * KC, KC, False, win_cons, cache=wc_win[j], cname="wc_win", first=(tt == 0))
            phase_end()
            if is_s5:
                s5_phase(j)
            else:
                hgrn_phase(j, l)
            phase_begin()
            Dn = Dense(need_g=True)
            for tt in range(NT):
                if is_s5:
                    for kc in range(KC):
                        S.add("sp", lambda e, kc=kc, tt=tt, Dn=Dn: e.dma_start(out=Dn.aT[kc], in_=yT[kc][:, tslice(tt)]),
                              reads=[("yT", kc)], writes=[(Dn.aTk, kc)], dma_key=Dn.aTk + "ld%d" % (kc % 4))

                    def glu_cons(n, outs):
                        (pv, pvk), (pg, pgk) = outs
                        t, tk = Dn.tslot()
                        ms, mk = Dn.mslot()
                        S.add("act", lambda e: e.activation(out=t, in_=pg[:], func=AF.Sigmoid), reads=[pgk], writes=[tk])
                        S.add("dve", lambda e: e.tensor_tensor(out=ms, in0=pv[:], in1=t, op=ALU.mult), reads=[pvk, tk], writes=[mk])
                        out_chunk(Dn, n, KC - 1, ms, mk, already_sbuf=True)
                    linear(Dn, s5_wg[j], KC, KC, True, glu_cons, cache=wc_glu[j], cname="wc_glu", first=(tt == 0))
                else:
                    for kc in range(KC):
                        os_, ok = load_chunk(Dn, onT[kc][:, tslice(tt)], ("onT", kc))
                        gs, gk = load_chunk(Dn, qfv[3 * KC + kc][:, tslice(tt)], ("qfv", 3 * KC + kc))
                        S.add("act", lambda e, gs=gs: e.activation(out=gs, in_=gs, func=AF.Silu), reads=[gk], writes=[gk])
                        S.add("dve", lambda e, os_=os_, gs=gs, kc=kc, Dn=Dn: e.tensor_tensor(out=Dn.aT[kc], in0=os_, in1=gs, op=ALU.mult),
                              reads=[ok, gk], writes=[(Dn.aTk, kc)])

                    def wo_cons(n, outs):
                        (ps, pk), = outs
                        out_chunk(Dn, n, KC - 1, ps[:], pk)
                    linear(Dn, hg_wout[j], KC, KC, False, wo_cons, cache=wc_wout[j], cname="wc_wout", first=(tt == 0))
                resid_update(Dn, tt, l, 1, hT, "hT")
                ffn_tile(Dn, tt, l, l == DEPTH - 1)
            fkeys = [k for k in S.dma_count if k.endswith("st") and k.startswith("hs")] if l == DEPTH - 1 else ()
            phase_end(final=fkeys)
        build_program.stats = (len(S.all_ops), len(S.dma_count), {e: len(S.ops[e]) for e in ENGS})
    return nc


def tile_w(W):
    K, N = W.shape
    return np.ascontiguousarray(W.reshape(K // 128, 128, N // 128, 128).transpose(2, 1, 0, 3))


def host_consts(T):
    c = np.zeros((128, 5 * 128 + 8 + 1 + 512), np.float32)
    c[:, 0:128] = np.eye(128)
    sw = np.zeros((128, 128), np.float32)
    for p in range(64):
        sw[p, 64 + p] = 1
        sw[64 + p, p] = 1
    c[:, 128:256] = sw
    s = np.arange(128)[:, None]
    t = np.arange(128)[None, :]
    c[:, 256:384] = (s <= t)
    c[:, 384:512] = 1.0
    for g in range(8):
        c[g * 16:(g + 1) * 16, 640 + g] = 1.0
    c[0:64, 648] = 1.0
    c[64:128, 648] = -1.0
    c[:, 649:649 + 512] = np.arange(1, 513, dtype=np.float32)[None, :]
    cm = np.ones((128, T), np.float32)
    cm[:, 0::HG_CH] = 0.0
    return c, cm


def prep_shared(inp, cfg):
    D, KC, G, DEPTH, NS5, NHG = cfg.D, cfg.KC, cfg.G, cfg.DEPTH, cfg.NS5, cfg.NHG
    f = lambda a: np.ascontiguousarray(np.asarray(a, dtype=np.float32))
    m = {}
    ng = f(inp["norm_gains"])
    m["gains"] = f(ng.reshape(DEPTH, 4, KC, 128).transpose(3, 0, 1, 2).reshape(128, DEPTH * 4 * KC))
    c, cm = host_consts(cfg.T)
    m["cst"], m["cmask"] = c, cm
    lbl = f(inp["hgrn_lb_logits"])
    m["lbl"] = f(lbl.reshape(DEPTH, KC, 128).transpose(2, 0, 1).reshape(128, DEPTH * KC))
    are, aim, ldt = f(inp["s5_a_re"]), f(inp["s5_a_im"]), f(inp["s5_log_dt"])
    a1 = np.zeros((NS5, 3, 128, G), np.float32)
    a1[:, 0] = np.concatenate([are.transpose(0, 2, 1)] * 2, axis=1)
    a1[:, 1] = np.concatenate([aim.transpose(0, 2, 1)] * 2, axis=1)
    a1[:, 2] = np.broadcast_to(ldt[:, None, :], (NS5, 128, G))
    m["s5_a1"] = a1
    a2 = np.zeros((NS5, KC, 3, 128, 64), np.float32)
    rep = lambda z: np.broadcast_to(z.reshape(NS5, KC, 8, 1, 64), (NS5, KC, 8, 16, 64)).reshape(NS5, KC, 128, 64)
    a2[:, :, 0] = rep(are)
    a2[:, :, 1] = rep(aim)
    a2[:, :, 2] = rep(np.broadcast_to(ldt[:, :, None], (NS5, G, 64)))
    m["s5_a2"] = a2
    bre, bim = f(inp["s5_b_re"]), f(inp["s5_b_im"])
    bt = lambda z: z.reshape(NS5, KC, 8, 64, 16).transpose(0, 1, 2, 4, 3).reshape(NS5, KC, 128, 64)
    m["s5_b2"] = f(np.stack([bt(bre), bt(bim)], axis=2))
    cre, cim = f(inp["s5_c_re"]), f(inp["s5_c_im"])
    ct = lambda z: z.reshape(NS5, KC, 8, 16, 64).transpose(0, 1, 4, 2, 3).reshape(NS5, KC, 64, 128)
    c0 = np.concatenate([ct(cre), ct(cim)], axis=2)
    c1 = np.concatenate([ct(cim), ct(cre)], axis=2)
    m["s5_c2"] = f(np.stack([c0, c1], axis=2))
    m["s5_dd"] = f(f(inp["s5_d"]).reshape(NS5, KC, 128).transpose(0, 2, 1))
    wg = f(inp["s5_w_glu"])
    m["s5_wg"] = f(np.stack([np.stack([tile_w(wg[j][:, :D]), tile_w(wg[j][:, D:])], axis=3) for j in range(NS5)]))
    m["hg_win"] = f(np.stack([tile_w(f(inp["hgrn_w_in"][j])) for j in range(NHG)]))
    m["hg_wout"] = f(np.stack([tile_w(f(inp["hgrn_w_out"][j])) for j in range(NHG)]))
    m["hg_gn"] = f(f(inp["hgrn_g_norm"]).reshape(NHG, KC, 128).transpose(0, 2, 1))
    gu = inp["ffn_w_gate_up"]
    DFF = cfg.DFF
    m["ff_gu"] = f(np.stack([np.stack([tile_w(f(gu[l][:, :DFF])), tile_w(f(gu[l][:, DFF:]))], axis=3) for l in range(DEPTH)]))
    m["ff_dn"] = f(np.stack([tile_w(f(inp["ffn_w_down"][l])) for l in range(DEPTH)]))
    return m


_CACHE = {}


def run(inp, cfg):
    x = np.asarray(inp["x"], dtype=np.float32)
    B = x.shape[0]
    shared = prep_shared(inp, cfg)
    in_maps = []
    for b in range(B):
        m = dict(shared)
        m["xT"] = np.ascontiguousarray(x[b].T.reshape(cfg.KC, 128, cfg.T))
        in_maps.append(m)
    kk = (cfg.D, cfg.T, cfg.DFF, cfg.DEPTH)
    if kk not in _CACHE:
        _CACHE[kk] = build_program(cfg)
    nc = _CACHE[kk]
    res = run_bass_kernel_spmd(nc, in_maps, core_ids=list(range(B)))
    out = np.stack([res.results[b]["outT"].reshape(cfg.D, cfg.T).T for b in range(B)], axis=0)
    if DEBUG:
        run.dbg = res.results
    return np.ascontiguousarray(out.astype(np.float32))


def kernel(**inputs):
    cfg = Cfg(D=4096, T=4096, DFF=11008, DEPTH=4)
    return run(inputs, cfg)
```
